# Optimizing a Trainium2 kernel written in Bass

```python
import jax
import jax.numpy as jnp
from jax import lax
import numpy as np

D_MODEL = 1024
BATCH = 8
SEQ = 2048
DEPTH = 2
DEC_BATCH = 128
DEC_SEQ = 1
PAST_LEN = 16384
PAGE_SIZE = 128

N_BRANCH = 4
BR_W = D_MODEL // 2
LRU_HEADS = 8
LRU_HD = BR_W // LRU_HEADS
LRU_CONV = 4
LRU_C = 8.0
RWKV_HD = 64
RWKV_HEADS = BR_W // RWKV_HD
RWKV_RANK_W = D_MODEL // 16
RWKV_RANK_A = D_MODEL // 16
RWKV_SHIFT_W = 3 * BR_W + RWKV_RANK_W + RWKV_RANK_A
RWKV_DECAY_SCALE = 0.606531
RWKV_LNX_EPS = 64e-5
CHUNK = 128
GMLP_GROUPS = 8
CONF_K = 31
LN_EPS = 1e-5

OFF_LRU_X = 0
OFF_LRU_G = OFF_LRU_X + BR_W
OFF_RWKV = OFF_LRU_G + BR_W
OFF_RWKV_G = OFF_RWKV + RWKV_SHIFT_W
OFF_GMLP_U = OFF_RWKV_G + BR_W
OFF_GMLP_V = OFF_GMLP_U + BR_W
OFF_GMLP_G = OFF_GMLP_V + BR_W
OFF_CONF_A = OFF_GMLP_G + BR_W
OFF_CONF_B = OFF_CONF_A + BR_W
OFF_CONF_G = OFF_CONF_B + BR_W
OFF_MERGE = OFF_CONF_G + BR_W
N_IN = OFF_MERGE + N_BRANCH * D_MODEL

kernel_name = 'hybrid_rglru_rwkv7_gmlp_conformer_step'


def _layernorm(x, g, b, eps=LN_EPS):
    xf = x.astype(jnp.float32)
    mu = jnp.mean(xf, axis=-1, keepdims=True)
    var = jnp.mean(jnp.square(xf - mu), axis=-1, keepdims=True)
    return ((xf - mu) * lax.rsqrt(var + eps) * g + b).astype(x.dtype)


def _causal_dwconv(x, buf, w, b):
    k = w.shape[0]
    xp = jnp.concatenate([buf.astype(x.dtype), x], axis=1)
    y = lax.conv_general_dilated(xp, w[:, None, :].astype(x.dtype), window_strides=(1,),
                                 padding='VALID', dimension_numbers=('NWC', 'WIO', 'NWC'),
                                 feature_group_count=x.shape[-1])
    return y + b, xp[:, xp.shape[1] - (k - 1):]


def _rglru_branch(xb, conv_buf, h0, conv_w, conv_b, wr, br, wi, bi, lam):
    B, T, _ = xb.shape
    xc, new_buf = _causal_dwconv(xb, conv_buf, conv_w, conv_b)
    xh = xc.reshape(B, T, LRU_HEADS, LRU_HD)
    r = jax.nn.sigmoid(jnp.einsum('bthi,hij->bthj', xh, wr).reshape(B, T, BR_W) + br)
    i = jax.nn.sigmoid(jnp.einsum('bthi,hij->bthj', xh, wi).reshape(B, T, BR_W) + bi)
    log_a = -LRU_C * r.astype(jnp.float32) * jax.nn.softplus(-lam.astype(jnp.float32))
    a = jnp.exp(log_a)
    u = jnp.sqrt(-jnp.expm1(2.0 * log_a)) * (i * xc).astype(jnp.float32)
    u = u.at[:, 0].add(a[:, 0] * h0.astype(jnp.float32))

    def comb(left, right):
        return left[0] * right[0], right[0] * left[1] + right[1]

    _, h = lax.associative_scan(comb, (a, u), axis=1)
    return h.astype(xb.dtype), new_buf, h[:, -1]


def _rwkv7_branch(p, shift0, S0, mu, w0, ww, a0, wa, k_k, k_a, r_k, lnx_g, lnx_b):
    B, T, _ = p.shape
    f32 = jnp.float32
    prev = jnp.concatenate([shift0[:, None].astype(p.dtype), p[:, :-1]], axis=1)
    xs = p + (prev - p) * mu
    r, k, v, dw, da = jnp.split(xs, [BR_W, 2 * BR_W, 3 * BR_W, 3 * BR_W + RWKV_RANK_W], axis=-1)
    log_w = -RWKV_DECAY_SCALE * jax.nn.sigmoid((w0 + jnp.tanh(dw) @ ww).astype(f32))
    a = jax.nn.sigmoid((a0 + da @ wa).astype(f32))

    def hs(t):
        return t.astype(f32).reshape(B, T, RWKV_HEADS, RWKV_HD)

    def ph(t):
        return t.astype(f32).reshape(RWKV_HEADS, RWKV_HD)

    r, k, v, a, w = hs(r), hs(k), hs(v), hs(a), hs(jnp.exp(log_w))
    kk = k * ph(k_k)
    kk = kk / jnp.maximum(jnp.sqrt(jnp.sum(kk * kk, axis=-1, keepdims=True)), 1e-12)
    k = k * (1.0 + (a - 1.0) * ph(k_a))

    def step(S, inp):
        r_t, w_t, k_t, v_t, kk_t, a_t = inp
        sa = jnp.einsum('bhvk,bhk->bhv', S, -kk_t)
        S = (S * w_t[:, :, None, :] + sa[..., None] * (kk_t * a_t)[:, :, None, :]
             + v_t[..., None] * k_t[:, :, None, :])
        return S, jnp.einsum('bhvk,bhk->bhv', S, r_t)

    def tm(t):
        return jnp.moveaxis(t, 1, 0)

    S, o = lax.scan(step, S0.astype(f32), (tm(r), tm(w), tm(k), tm(v), tm(kk), tm(a)))
    o = jnp.moveaxis(o, 0, 1)
    mo = jnp.mean(o, axis=-1, keepdims=True)
    vo = jnp.mean(jnp.square(o - mo), axis=-1, keepdims=True)
    o = (o - mo) * lax.rsqrt(vo + RWKV_LNX_EPS) * ph(lnx_g) + ph(lnx_b)
    o = o + jnp.sum(r * k * r_k.astype(f32), axis=-1, keepdims=True) * v
    return o.reshape(B, T, BR_W).astype(p.dtype), p[:, -1], S


def _chunk_spatial(v, ws, bs):
    B, T, W = v.shape
    Tp = -(-T // CHUNK) * CHUNK
    vp = jnp.pad(v, ((0, 0), (0, Tp - T), (0, 0))).reshape(B, Tp // CHUNK, CHUNK, GMLP_GROUPS, W // GMLP_GROUPS)
    mask = jnp.tril(jnp.ones((CHUNK, CHUNK), dtype=bool))
    wm = jnp.where(mask[None], ws, jnp.zeros_like(ws))
    z = jnp.einsum('gts,bnsgc->bntgc', wm, vp) + bs.T[None, None, :, :, None]
    return z.reshape(B, Tp, W)[:, :T]


def _gmlp_branch(u, v, ln_g, ln_b, ws, bs):
    vn = _layernorm(v, ln_g, ln_b)
    return u * _chunk_spatial(vn, ws, bs), vn


def _conformer_branch(ga, gb, buf, dw_w, dw_b, ln_g, ln_b):
    glu = ga * jax.nn.sigmoid(gb)
    y, new_buf = _causal_dwconv(glu, buf, dw_w, dw_b)
    return jax.nn.silu(_layernorm(y, ln_g, ln_b)), new_buf


def _layer(x, c, lru_buf, lru_h, rw_shift, rw_S, cf_buf, lp, alpha):
    B, T, _ = x.shape
    mod = jax.nn.silu(c) @ lp['w_cond'] + lp['b_cond']
    shift, scale, gate = jnp.split(mod, 3, axis=-1)
    h = x * (1.0 + scale[:, None]) + shift[:, None]
    proj = h @ lp['w_in']

    def col(off, n):
        return proj[..., off:off + n]

    o_a, lru_buf, lru_h = _rglru_branch(col(OFF_LRU_X, BR_W), lru_buf, lru_h, lp['lru_conv_w'], lp['lru_conv_b'],
                                        lp['lru_wr'], lp['lru_br'], lp['lru_wi'], lp['lru_bi'], lp['lru_lambda'])
    o_b, rw_shift, rw_S = _rwkv7_branch(col(OFF_RWKV, RWKV_SHIFT_W), rw_shift, rw_S, lp['rwkv_mu'], lp['rwkv_w0'],
                                        lp['rwkv_ww'], lp['rwkv_a0'], lp['rwkv_wa'], lp['rwkv_kk'], lp['rwkv_ka'],
                                        lp['rwkv_rk'], lp['rwkv_lnx_g'], lp['rwkv_lnx_b'])
    o_c, v_rows = _gmlp_branch(col(OFF_GMLP_U, BR_W), col(OFF_GMLP_V, BR_W), lp['gmlp_ln_g'], lp['gmlp_ln_b'],
                               lp['gmlp_ws'], lp['gmlp_bs'])
    o_d, cf_buf = _conformer_branch(col(OFF_CONF_A, BR_W), col(OFF_CONF_B, BR_W), cf_buf, lp['conf_dw_w'],
                                    lp['conf_dw_b'], lp['conf_ln_g'], lp['conf_ln_b'])
    silu_gates = jax.nn.silu(jnp.stack([col(OFF_LRU_G, BR_W), col(OFF_RWKV_G, BR_W),
                                        col(OFF_GMLP_G, BR_W), col(OFF_CONF_G, BR_W)], axis=2))
    o = jnp.stack([o_a, o_b, o_c, o_d], axis=2) * silu_gates
    g = jax.nn.sigmoid(col(OFF_MERGE, N_BRANCH * D_MODEL).reshape(B, T, N_BRANCH, D_MODEL))
    m = jnp.sum(g * jnp.einsum('btnw,nwd->btnd', o, lp['w_branch']), axis=2)
    y = m @ lp['w_out'] + lp['b_out']
    x = _layernorm(alpha * x + gate[:, None] * y, lp['ln_g'], lp['ln_b'])
    return x, (lru_buf, lru_h, rw_shift, rw_S, cf_buf, v_rows)


def setup_inputs(seed: int = 0) -> dict:
    key = jax.random.key(seed)
    ks = iter(jax.random.split(key, 64))
    L = DEPTH
    beta = (8.0 * DEPTH) ** -0.25

    def nrm(shape, s=1.0):
        return s * jax.random.normal(next(ks), shape, jnp.float32)

    def near_one(shape, s=0.1):
        return 1.0 + nrm(shape, s)

    a_c = jax.random.uniform(next(ks), (L, BR_W), jnp.float32, 0.9, 0.999)
    s_l = a_c ** (1.0 / LRU_C)
    lru_lambda = jnp.log(s_l) - jnp.log1p(-s_l)
    rwkv_mu = jax.random.uniform(next(ks), (L, RWKV_SHIFT_W), jnp.float32)
    return {
        'x_prompt': nrm((BATCH, SEQ, D_MODEL)),
        'x_sample': nrm((DEC_BATCH, DEC_SEQ, D_MODEL)),
        'state_lru_conv': nrm((L, DEC_BATCH, LRU_CONV - 1, BR_W)),
        'state_lru_h': nrm((L, DEC_BATCH, BR_W), 0.5),
        'state_rwkv_shift': nrm((L, DEC_BATCH, RWKV_SHIFT_W)),
        'state_rwkv_S': nrm((L, DEC_BATCH, RWKV_HEADS, RWKV_HD, RWKV_HD)),
        'state_conf_conv': nrm((L, DEC_BATCH, CONF_K - 1, BR_W), 0.5),
        'c_prompt': nrm((BATCH, D_MODEL)),
        'c_sample': nrm((DEC_BATCH, D_MODEL)),
        'w_cond': nrm((L, D_MODEL, 3 * D_MODEL), 0.5 * D_MODEL ** -0.5),
        'b_cond': nrm((L, 3 * D_MODEL), 0.01),
        'w_in': nrm((L, D_MODEL, N_IN), D_MODEL ** -0.5),
        'lru_conv_w': nrm((L, LRU_CONV, BR_W), LRU_CONV ** -0.5),
        'lru_conv_b': nrm((L, BR_W), 0.01),
        'lru_wr': nrm((L, LRU_HEADS, LRU_HD, LRU_HD), LRU_HD ** -0.5),
        'lru_br': nrm((L, BR_W), 0.01),
        'lru_wi': nrm((L, LRU_HEADS, LRU_HD, LRU_HD), LRU_HD ** -0.5),
        'lru_bi': nrm((L, BR_W), 0.01),
        'lru_lambda': lru_lambda,
        'rwkv_mu': rwkv_mu,
        'rwkv_w0': nrm((L, BR_W)),
        'rwkv_ww': nrm((L, RWKV_RANK_W, BR_W), 0.5 * RWKV_RANK_W ** -0.5),
        'rwkv_a0': nrm((L, BR_W), 0.5),
        'rwkv_wa': nrm((L, RWKV_RANK_A, BR_W), 0.5 * RWKV_RANK_A ** -0.5),
        'rwkv_kk': 0.85 + nrm((L, BR_W), 0.05),
        'rwkv_ka': near_one((L, BR_W), 0.05),
        'rwkv_rk': nrm((L, RWKV_HEADS, RWKV_HD), 0.1),
        'rwkv_lnx_g': near_one((L, BR_W)),
        'rwkv_lnx_b': nrm((L, BR_W), 0.01),
        'gmlp_ln_g': near_one((L, BR_W)),
        'gmlp_ln_b': nrm((L, BR_W), 0.01),
        'gmlp_ws': nrm((L, GMLP_GROUPS, CHUNK, CHUNK), CHUNK ** -0.5),
        'gmlp_bs': near_one((L, GMLP_GROUPS, CHUNK)),
        'conf_dw_w': nrm((L, CONF_K, BR_W), CONF_K ** -0.5),
        'conf_dw_b': nrm((L, BR_W), 0.01),
        'conf_ln_g': near_one((L, BR_W)),
        'conf_ln_b': nrm((L, BR_W), 0.01),
        'w_branch': nrm((L, N_BRANCH, BR_W, D_MODEL), beta * BR_W ** -0.5),
        'w_out': nrm((L, D_MODEL, D_MODEL), beta * D_MODEL ** -0.5),
        'b_out': nrm((L, D_MODEL), 0.01),
        'ln_g': near_one((L, D_MODEL)),
        'ln_b': nrm((L, D_MODEL), 0.01),
    }


def reference(x_prompt, x_sample, state_lru_conv, state_lru_h, state_rwkv_shift, state_rwkv_S, state_conf_conv,
              c_prompt, c_sample, w_cond, b_cond, w_in, lru_conv_w, lru_conv_b, lru_wr, lru_br, lru_wi, lru_bi,
              lru_lambda, rwkv_mu, rwkv_w0, rwkv_ww, rwkv_a0, rwkv_wa, rwkv_kk, rwkv_ka, rwkv_rk, rwkv_lnx_g,
              rwkv_lnx_b, gmlp_ln_g, gmlp_ln_b, gmlp_ws, gmlp_bs, conf_dw_w, conf_dw_b, conf_ln_g, conf_ln_b,
              w_branch, w_out, b_out, ln_g, ln_b):
    alpha = (2.0 * DEPTH) ** 0.25
    dt = x_prompt.dtype
    sdt = state_rwkv_S.dtype
    nb = x_prompt.shape[0]
    xp, xs = x_prompt, x_sample
    outs_p, outs_s = [], []
    for l in range(DEPTH):
        lp = dict(w_cond=w_cond[l], b_cond=b_cond[l], w_in=w_in[l], lru_conv_w=lru_conv_w[l],
                  lru_conv_b=lru_conv_b[l], lru_wr=lru_wr[l], lru_br=lru_br[l], lru_wi=lru_wi[l],
                  lru_bi=lru_bi[l], lru_lambda=lru_lambda[l], rwkv_mu=rwkv_mu[l], rwkv_w0=rwkv_w0[l],
                  rwkv_ww=rwkv_ww[l], rwkv_a0=rwkv_a0[l], rwkv_wa=rwkv_wa[l], rwkv_kk=rwkv_kk[l],
                  rwkv_ka=rwkv_ka[l], rwkv_rk=rwkv_rk[l], rwkv_lnx_g=rwkv_lnx_g[l], rwkv_lnx_b=rwkv_lnx_b[l],
                  gmlp_ln_g=gmlp_ln_g[l], gmlp_ln_b=gmlp_ln_b[l], gmlp_ws=gmlp_ws[l], gmlp_bs=gmlp_bs[l],
                  conf_dw_w=conf_dw_w[l], conf_dw_b=conf_dw_b[l], conf_ln_g=conf_ln_g[l],
                  conf_ln_b=conf_ln_b[l], w_branch=w_branch[l], w_out=w_out[l], b_out=b_out[l],
                  ln_g=ln_g[l], ln_b=ln_b[l])
        xp, sp = _layer(xp, c_prompt,
                        jnp.zeros((nb, LRU_CONV - 1, BR_W), dt), jnp.zeros((nb, BR_W), dt),
                        jnp.zeros((nb, RWKV_SHIFT_W), dt), jnp.zeros((nb, RWKV_HEADS, RWKV_HD, RWKV_HD), dt),
                        jnp.zeros((nb, CONF_K - 1, BR_W), dt), lp, alpha)
        xs, ss = _layer(xs, c_sample, state_lru_conv[l], state_lru_h[l], state_rwkv_shift[l], state_rwkv_S[l],
                        state_conf_conv[l], lp, alpha)
        outs_p.append(sp)
        outs_s.append(ss)

    def stk(outs, i, dtype):
        return jnp.stack([o[i] for o in outs]).astype(dtype)

    return (xp, xs,
            stk(outs_p, 0, dt), stk(outs_s, 0, sdt),
            stk(outs_p, 1, dt), stk(outs_s, 1, sdt),
            stk(outs_p, 2, dt), stk(outs_s, 2, sdt),
            stk(outs_p, 3, dt), stk(outs_s, 3, sdt),
            stk(outs_p, 4, dt), stk(outs_s, 4, sdt),
            stk(outs_s, 5, sdt))
```

```python
import contextlib
import numpy as np
import concourse.bass as bass
import concourse.mybir as mybir
from concourse.bass_utils import run_bass_kernel_spmd
from concourse.alu_op_type import AluOpType as ALU

F32 = mybir.dt.float32
BF16 = mybir.dt.bfloat16
AF = mybir.ActivationFunctionType
AX = mybir.AxisListType

N_DSEM = 24
SEQ = 2048
ST = 512
NST = SEQ // ST
NS = 16
NSC = 33
D = 1024
BW = 512
DEPTH = 2
SHW = 1664
CDEC = 0.606531
ALPHA = (2.0 * DEPTH) ** 0.25
OFF = dict(LRU_X=0, LRU_G=512, RWKV=1024, RWKV_G=2688, GM_U=3200, GM_V=3712, GM_G=4224,
           CF_A=4736, CF_B=5248, CF_G=5760, MERGE=6272)
WCOL = 544


def _key(x):
    if isinstance(x, str):
        return x
    t = getattr(x, 'tensor', None)
    if t is not None:
        return t.name
    return x.name


class Sched:
    def __init__(self, nc):
        self.nc = nc
        self.stack = contextlib.ExitStack()
        self.eng = {'pe': nc.tensor, 'act': nc.scalar, 'dve': nc.vector,
                    'pool': nc.gpsimd, 'sp': nc.sync}
        self.sem = {e: self.stack.enter_context(nc.semaphore("s_" + e)) for e in self.eng}
        self.cnt = {e: 0 for e in self.eng}
        self.dsem = [self.stack.enter_context(nc.semaphore("d%d" % i)) for i in range(N_DSEM)]
        self.dcnt = [0] * N_DSEM
        self.dnext = 0
        self.waited = {e: {} for e in self.eng}
        self.bufs = {}
        self.n_ops = 0

    def sb(self, name, shape, dt):
        return self.stack.enter_context(self.nc.sbuf_tensor(name, list(shape), dt))

    def ps(self, name, shape, dt):
        return self.stack.enter_context(self.nc.psum_tensor(name, list(shape), dt))

    def _need(self, e, r, w):
        need = []
        for k in r:
            b = self.bufs.get(k)
            if b and b['w'] is not None:
                need.append(b['w'])
        for k in w:
            b = self.bufs.get(k)
            if b:
                if b['w'] is not None:
                    need.append(b['w'])
                need.extend(b['r'])
        en = self.eng[e]
        for t in need:
            kind, idx, val = t
            if kind == 'eng' and idx == 'pe' and e == 'pe':
                continue
            key = (kind, idx)
            if self.waited[e].get(key, 0) >= val:
                continue
            self.waited[e][key] = val
            s = self.sem[idx] if kind == 'eng' else self.dsem[idx]
            en.wait_ge(s, val)

    def _record(self, ticket, r, w):
        for k in w:
            self.bufs[k] = {'w': ticket, 'r': []}
        for k in r:
            if k in w:
                continue
            b = self.bufs.setdefault(k, {'w': None, 'r': []})
            b['r'] = [t for t in b['r'] if not (t[0] == ticket[0] and t[1] == ticket[1])]
            b['r'].append(ticket)

    def op(self, e, fn, ins=(), outs=()):
        r = [_key(x) for x in ins]
        w = [_key(x) for x in outs]
        self._need(e, r, w)
        ins_ = fn(self.eng[e])
        self.cnt[e] += 1
        ins_.then_inc(self.sem[e], 1)
        self._record(('eng', e, self.cnt[e]), r, w)
        self.n_ops += 1
        return ins_

    def dma(self, q, out, in_, ins=(), outs=(), **kw):
        r = [_key(x) for x in ins]
        w = [_key(x) for x in outs]
        i = self.dnext
        self.dnext = (self.dnext + 1) % N_DSEM
        en = self.eng[q]
        if self.dcnt[i] > 0 and self.waited[q].get(('dma', i), 0) < self.dcnt[i]:
            self.waited[q][('dma', i)] = self.dcnt[i]
            en.wait_ge(self.dsem[i], self.dcnt[i])
        self._need(q, r, w)
        self.dcnt[i] += 16
        en.dma_start(out=out, in_=in_, **kw).then_inc(self.dsem[i], 16)
        t = ('dma', i, self.dcnt[i])
        self._record(t, r, w)
        return t

    def finish(self):
        en = self.eng['sp']
        for i in range(N_DSEM):
            if self.dcnt[i] > 0:
                en.wait_ge(self.dsem[i], self.dcnt[i])
        for e in self.eng:
            if e != 'sp' and self.cnt[e] > 0:
                en.wait_ge(self.sem[e], self.cnt[e])
        self.stack.close()


WEIGHT_NAMES = [
    ('w_cond', (2, 1024, 3072)), ('b_cond', (2, 3072)), ('w_in', (2, 1024, 10368)),
    ('lru_conv_w', (2, 4, 512)), ('lru_conv_b', (2, 512)), ('lru_wr', (2, 8, 64, 64)),
    ('lru_br', (2, 512)), ('lru_wi', (2, 8, 64, 64)), ('lru_bi', (2, 512)), ('lru_lambda', (2, 512)),
    ('rwkv_mu', (2, 1664)), ('rwkv_w0', (2, 512)), ('rwkv_ww', (2, 64, 512)), ('rwkv_a0', (2, 512)),
    ('rwkv_wa', (2, 64, 512)), ('rwkv_kk', (2, 512)), ('rwkv_ka', (2, 512)), ('rwkv_rk', (2, 8, 64)),
    ('rwkv_lnx_g', (2, 512)), ('rwkv_lnx_b', (2, 512)), ('gmlp_ln_g', (2, 512)), ('gmlp_ln_b', (2, 512)),
    ('gmlp_ws', (2, 8, 128, 128)), ('gmlp_bs', (2, 8, 128)), ('conf_dw_w', (2, 31, 512)),
    ('conf_dw_b', (2, 512)), ('conf_ln_g', (2, 512)), ('conf_ln_b', (2, 512)),
    ('w_branch', (2, 4, 512, 1024)), ('w_out', (2, 1024, 1024)), ('b_out', (2, 1024)),
    ('ln_g', (2, 1024)), ('ln_b', (2, 1024)),
]
STATE_NAMES = [('xp', (SEQ, D)), ('xs', (NS, D)), ('c33', (NSC, D)),
               ('st_lc', (2, NS, 3, 512)), ('st_lh', (2, NS, 512)), ('st_sh', (2, NS, SHW)),
               ('st_S', (2, NS, 8, 64, 64)), ('st_cf', (2, NS, 30, 512))]
OUT_NAMES = [('o_yp', (SEQ, D)), ('o_ys', (NS, D)), ('o_lcp', (2, 3, 512)), ('o_lcs', (2, NS, 3, 512)),
             ('o_lhp', (2, 512)), ('o_lhs', (2, NS, 512)), ('o_shp', (2, SHW)), ('o_shs', (2, NS, SHW)),
             ('o_Sp', (2, 8, 64, 64)), ('o_Ss', (2, NS, 8, 64, 64)), ('o_cfp', (2, 30, 512)),
             ('o_cfs', (2, NS, 30, 512)), ('o_gv', (2, NS, 512))]


def build_nc():
    nc = bass.Bass("TRN2", target_bir_lowering=False)
    S = Sched(nc)
    I = {}
    for n, shp in WEIGHT_NAMES + STATE_NAMES:
        I[n] = nc.dram_tensor(n, list(shp), F32, kind="ExternalInput").ap()
    O = {}
    for n, shp in OUT_NAMES:
        O[n] = nc.dram_tensor(n, list(shp), F32, kind="ExternalOutput").ap()
    scr_in = nc.dram_tensor("scr_in", [6, NS, 512], F32, kind="Internal").ap()
    scr_o = nc.dram_tensor("scr_o", [NS * 8, 64], F32, kind="Internal").ap()

    def _aps(*xs):
        return [x for x in xs if x is not None and not isinstance(x, (int, float))]

    def ACT(out, in_, func, bias=None, scale=None):
        kw = {}
        if bias is not None:
            kw['bias'] = bias
        if scale is not None:
            kw['scale'] = scale
        S.op('act', lambda e: e.activation(out=out, in_=in_, func=func, **kw),
             ins=_aps(in_, bias, scale), outs=[out])

    def TT(out, a, b, op, eng='dve'):
        S.op(eng, lambda e: e.tensor_tensor(out=out, in0=a, in1=b, op=op), ins=[a, b], outs=[out])

    def TS(out, a, s1, s2, op0, op1=None, eng='dve'):
        if op1 is None:
            S.op(eng, lambda e: e.tensor_scalar(out, a, s1, None, op0), ins=_aps(a, s1), outs=[out])
        else:
            S.op(eng, lambda e: e.tensor_scalar(out, a, s1, s2, op0, op1), ins=_aps(a, s1, s2), outs=[out])

    def STT(out, a, s, b, op0, op1):
        S.op('dve', lambda e: e.scalar_tensor_tensor(out=out, in0=a, scalar=s, in1=b, op0=op0, op1=op1),
             ins=_aps(a, s, b), outs=[out])

    def CP(out, in_, eng='dve'):
        if eng == 'act':
            S.op('act', lambda e: e.copy(out, in_), ins=[in_], outs=[out])
        else:
            S.op(eng, lambda e: e.tensor_copy(out, in_), ins=[in_], outs=[out])

    def MM(out, lhsT, rhs, start=True, stop=True):
        S.op('pe', lambda e: e.matmul(out, lhsT, rhs, start=start, stop=stop), ins=[lhsT, rhs], outs=[out])

    def TR(out, in_, n):
        S.op('pe', lambda e: e.transpose(out, in_, ident[0:n, 0:n]), ins=[in_, ident], outs=[out])

    def RED(out, in_, op=ALU.add):
        S.op('dve', lambda e: e.tensor_reduce(out=out, in_=in_, axis=AX.X, op=op), ins=[in_], outs=[out])

    def RECIP(out, in_):
        S.op('dve', lambda e: e.reciprocal(out, in_), ins=[in_], outs=[out])

    def MEMSET(ap, v, eng='pool'):
        S.op(eng, lambda e: e.memset(ap, v), ins=[], outs=[ap])

    def DMA(out, in_, q=None, **kw):
        in_dram = in_.tensor.name in D_names
        out_dram = out.tensor.name in D_names
        ins = [] if in_dram else [in_]
        outs = [] if out_dram else [out]
        if q is None:
            q = 'pool' if out_dram else 'sp'
        S.dma(q, out, in_, ins=ins, outs=outs, **kw)

    D_names = set(I.keys()) | set(O.keys())

    psum = [S.ps("pb%d" % i, [128, 512], F32) for i in range(8)]
    pstate = {'i': 0}

    def PS():
        t = psum[pstate['i'] % 8]
        pstate['i'] += 1
        return t

    ident = S.sb("ident", [128, 128], F32)
    identb = S.sb("identb", [128, 128], BF16)
    ones = S.sb("ones", [128, 128], F32)
    bones = S.sb("bones", [128, 128], F32)
    mask4 = S.sb("mask4", [128, 512], BF16)
    mls = S.sb("mls", [128, 128], BF16)

    def SEL(out, in_, pattern, cm, op, fill=0.0):
        S.op('pool', lambda e: e.affine_select(out=out, in_=in_, pattern=pattern, compare_op=op, fill=fill,
                                               base=0, channel_multiplier=cm), ins=[in_], outs=[out])

    Xb = [S.sb("X%d" % i, [128, D], F32) for i in range(4)]
    XS = S.sb("XS", [NS, D], F32)
    hT = S.sb("hT", [128, 8, 528], BF16)
    mT = S.sb("mT", [128, 8, 528], BF16)
    oT = S.sb("oT", [128, 4, 528], BF16)
    SG = S.sb("SG", [128, 4, 528], BF16)
    PJ = [S.sb("PJ%d" % i, [128, WCOL], F32) for i in range(13)]
    PJs = S.sb("PJs", [128, 13, NS], F32)
    T = [S.sb("T%d" % i, [128, WCOL], F32) for i in range(11)]
    TB = [S.sb("TB%d" % i, [128, WCOL], BF16) for i in range(4)]
    SM = S.sb("SM", [128, 64], F32)
    mtmp = T[0][:, 0:128]
    MEMSET(ident[:], 0.0)
    SEL(ident[:], ident[:], [[-1, 128]], 1, ALU.not_equal, 1.0)
    CP(identb[:], ident[:])
    MEMSET(ones[:], 1.0)
    MEMSET(bones[:], 0.0)
    MEMSET(bones[0:64, 0:64], 1.0)
    MEMSET(bones[64:128, 64:128], 1.0)
    MEMSET(mtmp, 1.0)
    SEL(mtmp, mtmp, [[1, 128]], -1, ALU.is_gt)
    CP(mask4[:, 0:128], mtmp)
    CP(mask4[:, 256:384], mtmp)
    MEMSET(mtmp, 1.0)
    SEL(mtmp, mtmp, [[1, 128]], -1, ALU.is_ge)
    CP(mask4[:, 128:256], mtmp)
    CP(mask4[:, 384:512], mtmp)
    MEMSET(mtmp, 1.0)
    SEL(mtmp, mtmp, [[-1, 128]], 1, ALU.is_gt)
    CP(mls[:], mtmp)

    SCT = S.sb("SCT", [128, 8, NSC], BF16)
    SCPR = S.sb("SCPR", [128, 8, 128], BF16)
    modT = [S.sb("modT%d" % l, [128, 16, NSC], F32) for l in range(2)]
    G33 = [S.sb("G33_%d" % l, [NSC, D], F32) for l in range(2)]
    LCH = [S.sb("LCH%d" % l, [128, 4, 3], F32) for l in range(2)]
    LH = [S.sb("LH%d" % l, [128, 4], F32) for l in range(2)]
    SHC = [S.sb("SHC%d" % l, [128, 13], F32) for l in range(2)]
    SIG = [S.sb("SIG%d" % l, [128, 4, 128], F32) for l in range(2)]
    CGC = [S.sb("CGC%d" % l, [128, 4, 30], F32) for l in range(2)]
    SIGf = S.sb("SIGf", [128, 128], BF16)
    for l in range(2):
        MEMSET(LCH[l][:], 0.0)
        MEMSET(LH[l][:], 0.0)
        MEMSET(SHC[l][:], 0.0)
        MEMSET(SIG[l][:], 0.0)
        MEMSET(CGC[l][:], 0.0)

    PST = S.sb("PST", [128, 128], F32)
    CST = S.sb("CST", [124, 128], F32)
    BRt = S.sb("BRt", [128, 128], F32)

    class _Cur:
        pass

    LP = _Cur()
    LPAR = []
    for l_ in range(2):
        d_ = dict(PP=S.sb("PP%d" % l_, [128, 128], F32), CW=S.sb("CW%d" % l_, [128, 124], F32),
                  WRI=S.sb("WRI%d" % l_, [128, 8, 128], BF16), WW=S.sb("WW%d" % l_, [128, 512], BF16),
                  wmT=S.sb("wmT%d" % l_, [128, 8, 128], BF16), GBt=S.sb("GBt%d" % l_, [128, 4, 128], F32),
                  GS0=S.sb("GS0%d" % l_, [128, 8], F32), DER=S.sb("DER%d" % l_, [128, 16], F32))
        LPAR.append(d_)

    def set_layer(l):
        for k_, v_ in LPAR[l].items():
            setattr(LP, k_, v_)

    PPC = {}
    _r = 0
    for nm, rows in [('lcw', 16), ('lcb', 4), ('lbr', 4), ('lbi', 4), ('llam', 4), ('mu', 13), ('w0', 4),
                     ('a0', 4), ('kk', 4), ('ka', 4), ('rk', 4), ('lxg', 4), ('lxb', 4), ('cfb', 4),
                     ('cfg', 4), ('cfbb', 4), ('bc', 16)]:
        PPC[nm] = _r
        _r += rows
    assert _r <= 128

    def pp(nm, i=0):
        c = PPC[nm] + i
        return LP.PP[:, c:c + 1]

    def rows128(ap1d):
        return ap1d.rearrange("(j p) -> j p", p=128)

    def load_params(l):
        set_layer(l)
        W = I
        MEMSET(PST[:], 0.0)
        for nm, src in [('lcw', W['lru_conv_w'][l].rearrange("k (j p) -> (k j) p", p=128)),
                        ('lcb', rows128(W['lru_conv_b'][l])), ('lbr', rows128(W['lru_br'][l])),
                        ('lbi', rows128(W['lru_bi'][l])), ('llam', rows128(W['lru_lambda'][l])),
                        ('mu', rows128(W['rwkv_mu'][l])), ('w0', rows128(W['rwkv_w0'][l])),
                        ('a0', rows128(W['rwkv_a0'][l])), ('kk', rows128(W['rwkv_kk'][l])),
                        ('ka', rows128(W['rwkv_ka'][l])),
                        ('rk', W['rwkv_rk'][l].rearrange("(j a) k -> j (a k)", a=2)),
                        ('lxg', rows128(W['rwkv_lnx_g'][l])), ('lxb', rows128(W['rwkv_lnx_b'][l])),
                        ('cfb', rows128(W['conf_dw_b'][l])), ('cfg', rows128(W['conf_ln_g'][l])),
                        ('cfbb', rows128(W['conf_ln_b'][l])), ('bc', rows128(W['b_cond'][l, 0:2048]))]:
            n = src.shape[0]
            DMA(PST[PPC[nm]:PPC[nm] + n, :], src)
        ps = PS()
        TR(ps[:, 0:128], PST[:, :], 128)
        CP(LP.PP[:, :], ps[:, 0:128], 'act')
        for j in range(4):
            DMA(CST[j * 31:(j + 1) * 31, :], W['conf_dw_w'][l][:, j * 128:(j + 1) * 128])
        ps = PS()
        TR(ps[:, 0:124], CST[:, :], 124)
        CP(LP.CW[:, :], ps[:, 0:124], 'act')
        ACT(LP.DER[:, 0:4], LP.PP[:, PPC['llam']:PPC['llam'] + 4], AF.Exp, scale=-1.0)
        ACT(LP.DER[:, 0:4], LP.DER[:, 0:4], AF.Ln, bias=1.0)
        TS(LP.DER[:, 4:8], LP.DER[:, 0:4], -16.0, None, ALU.mult)
        TS(LP.DER[:, 0:4], LP.DER[:, 0:4], -8.0, None, ALU.mult)
        TS(LP.DER[:, 8:12], LP.PP[:, PPC['ka']:PPC['ka'] + 4], -1.0, 1.0, ALU.mult, ALU.add)
        for g, nm in enumerate(['lru_wr', 'lru_wi']):
            stg = T[1 + g][:, 0:512].rearrange("p (j o) -> p j o", j=4)
            MEMSET(T[1 + g][:, 0:512], 0.0)
            for h in range(2):
                src = W[nm][l].rearrange("(j a) i o -> a i j o", a=2)[h]
                DMA(stg[h * 64:(h + 1) * 64, :, h * 64:(h + 1) * 64], src)
            CP(LP.WRI[:, g * 4:(g + 1) * 4, :], stg)
        DMA(T[3][0:64, 0:512], W['rwkv_ww'][l])
        DMA(T[3][64:128, 0:512], W['rwkv_wa'][l])
        CP(LP.WW[:], T[3][:, 0:512])
        for g in range(8):
            DMA(T[0][:, 0:128], W['gmlp_ws'][l, g])
            ps = PS()
            TR(ps[:, 0:128], T[0][:, 0:128], 128)
            TT(LP.wmT[:, g, :], ps[:, 0:128], mask4[:, 128:256], ALU.mult)
        for h in range(2):
            src = W['gmlp_bs'][l].rearrange("(j a) t -> a j t", a=2)[h]
            DMA(LP.GBt[h * 64:(h + 1) * 64, :, :], src.partition_broadcast(64))
            s2 = W['gmlp_ws'][l].rearrange("(j a) t s -> a j (t s)", a=2)[h][:, 0]
            DMA(LP.GS0[h * 64:(h + 1) * 64, 0:4], s2.partition_broadcast(64), allow_slow_non_contiguous=True)
            s3 = W['gmlp_bs'][l].rearrange("(j a) t -> a j t", a=2)[h][:, 0]
            DMA(LP.GS0[h * 64:(h + 1) * 64, 4:8], s3.partition_broadcast(64), allow_slow_non_contiguous=True)

    NSTG, NSLOT = 2, 5
    NBLK_MAX = 140
    wstg = [S.sb("wstg%d" % i, [128, 2048], F32) for i in range(NSTG)]
    wslot = [S.sb("wslot%d" % i, [128, 2048], BF16) for i in range(NSLOT)]
    wscr = [nc.dram_tensor("wscr%d" % l, [NBLK_MAX, 128, 2048], BF16, kind="Internal").ap() for l in range(2)]
    cast_rr = ['act', 'dve']

    class WStream:
        def __init__(self):
            self.sched = []
            self.dma_i = 0
            self.cast_i = 0
            self.get_i = 0
            self.nstage = 0

        def extend(self, items):
            self.sched.extend(items)

        def _dma(self):
            i = self.dma_i
            if i >= len(self.sched):
                return
            name, src, a, b, mode, scr = self.sched[i]
            if mode == 'direct':
                l_, bi = scr
                S.dma('sp', wslot[i % NSLOT][:, 0:a * b], wscr[l_][bi][:, 0:a * b],
                      ins=['wscr%d_%d' % (l_, bi)], outs=[wslot[i % NSLOT]])
            else:
                st = wstg[self.nstage % NSTG]
                self.sched[i] = (name, src, a, b, mode, scr, self.nstage % NSTG)
                self.nstage += 1
                S.dma('sp', st[:, 0:a * b].rearrange("p (a b) -> p a b", a=a), src, ins=[], outs=[st])
            self.dma_i += 1

        def _cast(self):
            i = self.cast_i
            if i >= len(self.sched):
                return
            while self.dma_i <= i:
                self._dma()
            it = self.sched[i]
            name, src, a, b, mode, scr = it[:6]
            if mode != 'direct':
                CP(wslot[i % NSLOT][:, 0:a * b], wstg[it[6]][:, 0:a * b], cast_rr[i % 2])
                if scr is not None:
                    l_, bi = scr
                    S.dma('pool', wscr[l_][bi][:, 0:a * b], wslot[i % NSLOT][:, 0:a * b],
                          ins=[wslot[i % NSLOT]], outs=['wscr%d_%d' % (l_, bi)])
            self.cast_i += 1

        def get(self, name):
            i = self.get_i
            assert self.sched[i][0] == name, (self.sched[i][0], name)
            while self.cast_i <= i:
                self._cast()
            while self.dma_i <= i + 2:
                if self.dma_i >= len(self.sched):
                    break
                self._dma()
            if self.cast_i <= i + 1:
                self._cast()
            name, src, a, b = self.sched[i][:4]
            self.get_i += 1
            return wslot[i % NSLOT][:, 0:a * b].rearrange("p (a b) -> p a b", a=a)

    WS = WStream()

    def win_blk(l, col0, ncols=256):
        return (I['w_in'][l][:, col0:col0 + ncols].rearrange("(kc p) n -> p kc n", p=128), 8, ncols)

    def sched_layer_st(l, st):
        items = []
        pre = []
        if st == 0:
            for i in range(12):
                pre.append(("wc%d" % i,
                            I['w_cond'][l][:, i * 256:(i + 1) * 256].rearrange("(kc p) n -> p kc n", p=128), 8, 256,
                            'stage', None))

        def add_in(tag, off, n):
            for i in range(n // 256):
                src, a, b = win_blk(l, off + i * 256)
                items.append(("%s%d" % (tag, i), src, a, b))

        def add_merge(n):
            for half in range(2):
                src = I['w_branch'][l, n][:, half * 512:(half + 1) * 512].rearrange("(kc p) n -> p kc n", p=128)
                items.append(("wb%d_%d" % (n, half), src, 4, 512))
                for q in range(2):
                    src, a, b = win_blk(l, OFF['MERGE'] + n * 1024 + (half * 2 + q) * 256)
                    items.append(("mg%d_%d" % (n, half * 2 + q), src, a, b))

        add_in('lx', OFF['LRU_X'], 512)
        add_in('lg', OFF['LRU_G'], 512)
        add_merge(0)
        add_in('rw', OFF['RWKV'], 1536)
        src, a, b = win_blk(l, OFF['RWKV'] + 1536, 128)
        items.append(('rwlo', src, a, b))
        add_in('rg', OFF['RWKV_G'], 512)
        add_merge(1)
        add_in('gu', OFF['GM_U'], 512)
        add_in('gv', OFF['GM_V'], 512)
        add_in('gg', OFF['GM_G'], 512)
        add_merge(2)
        add_in('ca', OFF['CF_A'], 512)
        add_in('cb', OFF['CF_B'], 512)
        add_in('cg', OFF['CF_G'], 512)
        add_merge(3)
        for i in range(4):
            src = I['w_out'][l][:, i * 256:(i + 1) * 256].rearrange("(kc p) n -> p kc n", p=128)
            items.append(("wo%d" % i, src, 8, 256))
        out = []
        for bi, (nm, src, a, b) in enumerate(items):
            assert bi < NBLK_MAX
            out.append((nm, src, a, b, 'stage' if st == 0 else 'direct', (l, bi)))
        return pre + out

    for st in range(NST):
        for l in range(2):
            WS.extend(sched_layer_st(l, st))

    def cond_setup():
        c = T[0]
        DMA(c[0:NSC, 0:512], I['c33'][:, 0:512])
        DMA(T[1][0:NSC, 0:512], I['c33'][:, 512:1024])
        ACT(c[0:NSC, 0:512], c[0:NSC, 0:512], AF.Silu)
        ACT(T[1][0:NSC, 0:512], T[1][0:NSC, 0:512], AF.Silu)
        for kc in range(8):
            src = (c if kc < 4 else T[1])[0:NSC, (kc % 4) * 128:(kc % 4 + 1) * 128]
            ps = PS()
            TR(ps[:, 0:NSC], src, NSC)
            CP(SCT[:, kc, :], ps[:, 0:NSC], 'act')
            CP(SCPR[:, kc, :], SCT[:, kc, 32:33].broadcast_to([128, 128]))

    def cond_layer(l):
        DMA(T[2][0:NSC, 0:512], I['b_cond'][l:l + 1, 2048:2560].broadcast_to([NSC, 512]))
        DMA(T[3][0:NSC, 0:512], I['b_cond'][l:l + 1, 2560:3072].broadcast_to([NSC, 512]))
        import os
        ksub = int(os.environ.get('KSUB', '12'))
        for i in range(12):
            if i >= ksub:
                return
            wb = WS.get("wc%d" % i)
            if i < 8:
                for ci in range(2):
                    ch = i * 2 + ci
                    ps = PS()
                    for kc in range(8):
                        MM(ps[:, 0:NSC], wb[:, kc, ci * 128:(ci + 1) * 128], SCT[:, kc, :], kc == 0, kc == 7)
                    if ch < 8:
                        TS(modT[l][:, ch, :], ps[:, 0:NSC], pp('bc', ch), None, ALU.add)
                    else:
                        TS(modT[l][:, ch, :], ps[:, 0:NSC], pp('bc', ch), 1.0, ALU.add, ALU.add)
            else:
                g0 = (i - 8) * 256
                ps = PS()
                for kc in range(8):
                    MM(ps[0:NSC, 0:256], SCT[:, kc, :], wb[:, kc, :], kc == 0, kc == 7)
                bt = (T[2] if g0 < 512 else T[3])[0:NSC, (g0 % 512):(g0 % 512) + 256]
                TT(G33[l][:, g0:g0 + 256], ps[0:NSC, 0:256], bt, ALU.add)

    def proj_fm(wb, ci, n, c0):
        ps = PS()
        for kc in range(8):
            MM(ps[:, 0:n], wb[:, kc, ci * 128:(ci + 1) * 128], hT[:, kc, c0:c0 + n], kc == 0, kc == 7)
        return ps

    def fm_to_rows(srcs, n, dst):
        ps = PS()
        for j, s in enumerate(srcs):
            TR(ps[0:n, j * 128:(j + 1) * 128], s, 128)
        CP(dst, ps[0:n, 0:512], 'act')

    evac_rr = {'i': 0}

    def EV(out, in_):
        e = 'act' if evac_rr['i'] % 2 == 0 else 'dve'
        evac_rr['i'] += 1
        CP(out, in_, e)

    def gate_stage(tag, samp):
        for b in range(2):
            wb = WS.get("%s%d" % (tag, b))
            for ci in range(2):
                j = b * 2 + ci
                ps = proj_fm(wb, ci, ST, 0)
                ACT(SG[:, j, 0:ST], ps[:, 0:ST], AF.Silu)
                if samp:
                    ps = proj_fm(wb, ci, NS, ST)
                    ACT(SG[:, j, ST:ST + NS], ps[:, 0:NS], AF.Silu)

    def merge_stage(n, samp):
        for half in range(2):
            wbr = WS.get("wb%d_%d" % (n, half))
            for q in range(2):
                wm = WS.get("mg%d_%d" % (n, half * 2 + q))
                for ci in range(2):
                    dc = half * 4 + q * 2 + ci
                    for (c0, nn) in ([(0, ST), (ST, NS)] if samp else [(0, ST)]):
                        psA = proj_fm(wm, ci, nn, c0)
                        psB = PS()
                        for kc in range(4):
                            MM(psB[:, 0:nn], wbr[:, kc, (dc % 4) * 128:(dc % 4 + 1) * 128], oT[:, kc, c0:c0 + nn],
                               kc == 0, kc == 3)
                        sg = T[9][:, 0:nn]
                        ACT(sg, psA[:, 0:nn], AF.Sigmoid)
                        if n == 0:
                            TT(mT[:, dc, c0:c0 + nn], sg, psB[:, 0:nn], ALU.mult)
                        else:
                            TT(T[10][:, 0:nn], sg, psB[:, 0:nn], ALU.mult)
                            TT(mT[:, dc, c0:c0 + nn], T[10][:, 0:nn], mT[:, dc, c0:c0 + nn], ALU.add, eng='pool')

    def build_hT(l, st, samp):
        if l == 0:
            for blk in range(4):
                DMA(Xb[blk][:, :], I['xp'][st * ST + blk * 128: st * ST + (blk + 1) * 128, :])
            if samp:
                DMA(XS[:, :], I['xs'])
        for blk in range(4):
            for kg in range(2):
                ps = PS()
                for q in range(4):
                    kc = kg * 4 + q
                    TR(ps[:, q * 128:(q + 1) * 128], Xb[blk][:, kc * 128:(kc + 1) * 128], 128)
                for q in range(4):
                    kc = kg * 4 + q
                    ACT(hT[:, kc, blk * 128:(blk + 1) * 128], ps[:, q * 128:(q + 1) * 128], AF.Identity,
                        bias=modT[l][:, kc, 32:33], scale=modT[l][:, 8 + kc, 32:33])
        if samp:
            for kc in range(8):
                ps2 = PS()
                TR(ps2[:, 0:NS], XS[:, kc * 128:(kc + 1) * 128], NS)
                TT(T[0][:, 0:NS], ps2[:, 0:NS], modT[l][:, 8 + kc, 0:NS], ALU.mult)
                TT(hT[:, kc, ST:ST + NS], T[0][:, 0:NS], modT[l][:, kc, 0:NS], ALU.add)

    def lru_branch(l, st, samp):
        for b in range(2):
            wb = WS.get("lx%d" % b)
            for ci in range(2):
                j = b * 2 + ci
                ps = proj_fm(wb, ci, ST, 0)
                EV(PJ[j][:, 3:3 + ST], ps[:, 0:ST])
                if samp:
                    ps = proj_fm(wb, ci, NS, ST)
                    EV(PJs[:, j, :], ps[:, 0:NS])
        gate_stage('lg', samp)
        if samp:
            LB = T[8]
            LBv = LB[:, 0:192].rearrange("p (j s k) -> p j s k", j=4, s=NS)
            DMA(T[7][0:48, 0:512], I['st_lc'][l].rearrange("s k c -> (s k) c"))
            DMA(T[6][0:NS, 0:512], I['st_lh'][l])
            ps = PS()
            ps2 = PS()
            for j in range(4):
                TR(ps[:, j * 48:(j + 1) * 48], T[7][0:48, j * 128:(j + 1) * 128], 48)
                TR(ps2[:, j * NS:(j + 1) * NS], T[6][0:NS, j * 128:(j + 1) * 128], NS)
            CP(LB[:, 0:192], ps[:, 0:192], 'act')
            CP(LB[:, 192:256], ps2[:, 0:64], 'act')
            H0v = LB[:, 192:256].rearrange("p (j s) -> p j s", j=4)
            S.dma('sp', O['o_lcs'][l][:, 0:2, :], I['st_lc'][l][:, 1:3, :], ins=[], outs=['o_lcs_a'])
        def parts(is_s, n, j, ts):
            xc_t, r_t, ig_t, a_t, a2_t, xcb_t = ts
            xc = xc_t[:, 0:n]
            xcb = xcb_t[:, 0:n]
            r = r_t[:, 0:n]
            ig = ig_t[:, 0:n]
            a = a_t[:, 0:n]
            a2 = a2_t[:, 0:n]
            h = r_t[:, 0:n]
            box = {}

            def A1():
                if not is_s:
                    xb = PJ[j]
                    CP(xb[:, 0:3], LCH[l][:, j, :], 'pool')
                    TS(xc, xb[:, 3:3 + n], pp('lcw', 12 + j), pp('lcb', j), ALU.mult, ALU.add)
                    for k in range(3):
                        STT(xc, xb[:, k:k + n], pp('lcw', k * 4 + j), xc, ALU.mult, ALU.add)
                    CP(LCH[l][:, j, :], xb[:, ST:ST + 3], 'pool')
                else:
                    TS(xc, PJs[:, j, :], pp('lcw', 12 + j), pp('lcb', j), ALU.mult, ALU.add)
                    for k in range(3):
                        STT(xc, LBv[:, j, :, k], pp('lcw', k * 4 + j), xc, ALU.mult, ALU.add)

            def A2():
                CP(xcb, xc, 'act')
                box['psr'] = PS()
                MM(box['psr'][:, 0:n], LP.WRI[:, j, :], xcb)
                box['psi'] = PS()
                MM(box['psi'][:, 0:n], LP.WRI[:, 4 + j, :], xcb)

            def B1():
                ACT(r, box['psr'][:, 0:n], AF.Sigmoid, bias=pp('lbr', j))
                ACT(ig, box['psi'][:, 0:n], AF.Sigmoid, bias=pp('lbi', j))
                ACT(a, r, AF.Exp, scale=LP.DER[:, j:j + 1])
                ACT(a2, r, AF.Exp, scale=LP.DER[:, 4 + j:5 + j])
                ACT(a2, a2, AF.Sqrt, bias=1.0, scale=-1.0)

            def B2():
                TT(ig, ig, a2, ALU.mult)
                TT(ig, ig, xc, ALU.mult)
                if not is_s:
                    xb = PJ[j]
                    S.op('dve', lambda e: e.tensor_tensor_scan(out=h, data0=a, data1=ig, initial=LH[l][:, j:j + 1],
                                                               op0=ALU.mult, op1=ALU.add),
                         ins=[a, ig, LH[l]], outs=[h])
                    CP(LH[l][:, j:j + 1], h[:, n - 1:n], 'pool')
                    if st == NST - 1:
                        DMA(O['o_lhp'][l, j * 128:(j + 1) * 128].rearrange("(p o) -> p o", o=1), h[:, n - 1:n])
                        CP(T[5][:, j * 3:(j + 1) * 3], xb[:, ST:ST + 3], 'pool')
                else:
                    TT(h, a, H0v[:, j, :], ALU.mult)
                    TT(h, h, ig, ALU.add)
                    CP(T[5][:, j * NS:(j + 1) * NS], h, 'pool')
                c0 = ST if is_s else 0
                TT(oT[:, j, c0:c0 + n], h, SG[:, j, c0:c0 + n], ALU.mult)

            return A1, A2, B1, B2

        tsets = [(T[0], T[1], T[2], T[3], T[4], TB[0]), (PJ[4], PJ[5], PJ[6], PJ[7], PJ[8], TB[1])]
        for (is_s, n) in ([(False, ST), (True, NS)] if samp else [(False, ST)]):
            if is_s:
                for j in range(4):
                    for f in parts(True, n, j, tsets[0]):
                        f()
            else:
                P = [parts(False, n, j, tsets[j % 2]) for j in range(4)]
                P[0][0]()
                P[0][1]()
                for j in range(4):
                    if j < 3:
                        P[j + 1][0]()
                    P[j][2]()
                    if j < 3:
                        P[j + 1][1]()
                    P[j][3]()
            if is_s:
                fm_to_rows([T[5][:, j * NS:(j + 1) * NS] for j in range(4)], NS, T[6][0:NS, 0:512])
                DMA(O['o_lhs'][l], T[6][0:NS, 0:512])
                fm_to_rows([PJs[:, j, :] for j in range(4)], NS, T[7][0:NS, 0:512])
                DMA(O['o_lcs'][l][:, 2, :], T[7][0:NS, 0:512])
            elif st == NST - 1:
                fm_to_rows([T[5][:, j * 3:(j + 1) * 3] for j in range(4)], 3, T[6][0:3, 0:512])
                DMA(O['o_lcp'][l], T[6][0:3, 0:512])
        merge_stage(0, samp)

    def conf_branch(l, st, samp):
        for b in range(2):
            wb = WS.get("ca%d" % b)
            for ci in range(2):
                j = b * 2 + ci
                ps = proj_fm(wb, ci, ST, 0)
                EV(PJ[j][:, 30:30 + ST], ps[:, 0:ST])
                if samp:
                    ps = proj_fm(wb, ci, NS, ST)
                    EV(PJs[:, j, :], ps[:, 0:NS])
        for b in range(2):
            wb = WS.get("cb%d" % b)
            for ci in range(2):
                j = b * 2 + ci
                ps = proj_fm(wb, ci, ST, 0)
                ACT(T[0][:, 0:ST], ps[:, 0:ST], AF.Sigmoid)
                TT(PJ[j][:, 30:30 + ST], PJ[j][:, 30:30 + ST], T[0][:, 0:ST], ALU.mult)
                if samp:
                    ps = proj_fm(wb, ci, NS, ST)
                    ACT(T[0][:, 0:NS], ps[:, 0:NS], AF.Sigmoid)
                    TT(PJs[:, j, :], PJs[:, j, :], T[0][:, 0:NS], ALU.mult)
        gate_stage('cg', samp)
        Y = [PJ[4 + j] for j in range(4)]
        Ys = PJs[:, 4:8, :]
        DW = AM[:, :, :].rearrange("p a b -> p (a b)")[:, 0:31 * 128].rearrange("p (k c) -> p k c", k=31)
        if samp:
            for g in range(4):
                DMA(T[g][0:120, 0:512], I['st_cf'][l][g * 4:(g + 1) * 4].rearrange("s k c -> (s k) c"))
            for j in range(4):
                ps = PS()
                for g in range(4):
                    TR(ps[:, g * 120:(g + 1) * 120], T[g][0:120, j * 128:(j + 1) * 128], 120)
                CP(PJ[8 + j][:, 0:480], ps[:, 0:480], 'act')
            S.dma('sp', O['o_cfs'][l][:, 0:29, :], I['st_cf'][l][:, 1:30, :], ins=[], outs=['o_cfs_a'])
        for j in range(4):
            gl = PJ[j]
            CP(gl[:, 0:30], CGC[l][:, j, :], 'pool')
            glb = TB[j % 2]
            CP(glb[:, 0:30 + ST], gl[:, 0:30 + ST], 'act')
            if j % 2 == 0:
                TT(DW, identb[:, :].unsqueeze(1).broadcast_to([128, 31, 128]),
                   LP.CW[:, j * 31:(j + 1) * 31].unsqueeze(2).broadcast_to([128, 31, 128]), ALU.mult)
                taps = [DW[:, k, :] for k in range(31)]
            else:
                dwa = PQ[0][:, :, :].rearrange("p a b -> p (a b)").rearrange("p (k c) -> p k c", k=16)
                dwb = PQ[1][:, :, :].rearrange("p a b -> p (a b)")[:, 0:15 * 128].rearrange("p (k c) -> p k c", k=15)
                TT(dwa, identb[:, :].unsqueeze(1).broadcast_to([128, 16, 128]),
                   LP.CW[:, j * 31:j * 31 + 16].unsqueeze(2).broadcast_to([128, 16, 128]), ALU.mult)
                TT(dwb, identb[:, :].unsqueeze(1).broadcast_to([128, 15, 128]),
                   LP.CW[:, j * 31 + 16:(j + 1) * 31].unsqueeze(2).broadcast_to([128, 15, 128]), ALU.mult)
                taps = [dwa[:, k, :] for k in range(16)] + [dwb[:, k, :] for k in range(15)]
            ps = PS()
            for k in range(31):
                MM(ps[:, 0:ST], taps[k], glb[:, k:k + ST], k == 0, k == 30)
            ACT(Y[j][:, 0:ST], ps[:, 0:ST], AF.Identity, bias=pp('cfb', j))
            CP(CGC[l][:, j, :], gl[:, ST:ST + 30], 'pool')
            if st == NST - 1:
                CP(T[5][:, j * 30:(j + 1) * 30], gl[:, ST:ST + 30], 'pool')
            if samp:
                cb = PJ[8 + j][:, 0:480].rearrange("p (s k) -> p s k", s=NS)
                wbc = LP.CW[:, j * 31:j * 31 + 30].unsqueeze(1).broadcast_to([128, NS, 30])
                tmp = T[0][:, 0:480].rearrange("p (s k) -> p s k", s=NS)
                TT(tmp, cb, wbc, ALU.mult)
                RED(T[1][:, 0:NS], tmp)
                STT(T[1][:, 0:NS], PJs[:, j, :], LP.CW[:, j * 31 + 30:j * 31 + 31], T[1][:, 0:NS], ALU.mult, ALU.add)
                TS(Ys[:, j, :], T[1][:, 0:NS], pp('cfb', j), None, ALU.add)
        if st == NST - 1:
            fm_to_rows([T[5][:, j * 30:(j + 1) * 30] for j in range(4)], 30, T[6][0:30, 0:512])
            DMA(O['o_cfp'][l], T[6][0:30, 0:512])
        if samp:
            fm_to_rows([PJs[:, j, :] for j in range(4)], NS, T[7][0:NS, 0:512])
            DMA(O['o_cfs'][l][:, 29, :], T[7][0:NS, 0:512])
        for (is_s, n) in ([(False, ST), (True, NS)] if samp else [(False, ST)]):
            ys = [(Ys[:, j, :] if is_s else Y[j][:, 0:n]) for j in range(4)]
            ps1 = PS()
            ps2 = PS()
            for j in range(4):
                MM(ps1[:, 0:n], ones[:], ys[j], j == 0, j == 3)
            for j in range(4):
                ACT(T[j][:, 0:n], ys[j], AF.Square)
            for j in range(4):
                MM(ps2[:, 0:n], ones[:], T[j][:, 0:n], j == 0, j == 3)
            mean = T[4][:, 0:n]
            rstd = T[5][:, 0:n]
            ACT(mean, ps1[:, 0:n], AF.Copy, scale=1.0 / 512)
            TT(T[6][:, 0:n], mean, mean, ALU.mult)
            STT(rstd, ps2[:, 0:n], 1.0 / 512, T[6][:, 0:n], ALU.mult, ALU.subtract)
            TS(rstd, rstd, 0.0, 1e-5, ALU.max, ALU.add)
            ACT(rstd, rstd, AF.Sqrt)
            RECIP(rstd, rstd)
            c0 = ST if is_s else 0
            for j in range(4):
                t = T[j][:, 0:n]
                TT(t, ys[j], mean, ALU.subtract)
                TT(t, t, rstd, ALU.mult)
                ACT(t, t, AF.Silu, bias=pp('cfbb', j), scale=pp('cfg', j))
                TT(oT[:, j, c0:c0 + n], t, SG[:, j, c0:c0 + n], ALU.mult)
        merge_stage(3, samp)

    def gmlp_branch(l, st, samp):
        for b in range(2):
            wb = WS.get("gu%d" % b)
            for ci in range(2):
                j = b * 2 + ci
                ps = proj_fm(wb, ci, ST, 0)
                EV(PJ[j][:, 0:ST], ps[:, 0:ST])
                if samp:
                    ps = proj_fm(wb, ci, NS, ST)
                    EV(PJs[:, j, :], ps[:, 0:NS])
        for b in range(2):
            wb = WS.get("gv%d" % b)
            for blk in range(4):
                ps = PS()
                for kc in range(8):
                    MM(ps[:, 0:256], hT[:, kc, blk * 128:(blk + 1) * 128], wb[:, kc, :], kc == 0, kc == 7)
                EV(PJ[4 + blk][:, b * 256:(b + 1) * 256], ps[:, 0:256])
            if samp:
                ps = PS()
                for kc in range(8):
                    MM(ps[0:NS, 0:256], hT[:, kc, ST:ST + NS], wb[:, kc, :], kc == 0, kc == 7)
                EV(PJ[8][0:NS, b * 256:(b + 1) * 256], ps[0:NS, 0:256])
        gate_stage('gg', samp)
        DMA(T[5][:, 0:512], I['gmlp_ln_g'][l:l + 1, :].broadcast_to([128, 512]))
        DMA(T[6][:, 0:512], I['gmlp_ln_b'][l:l + 1, :].broadcast_to([128, 512]))
        GLN = [T[5], T[6]]
        VN = TB[0:4]
        for blk in range(5 if samp else 4):
            np_ = 128 if blk < 4 else NS
            v = (PJ[4 + blk] if blk < 4 else PJ[8])[0:np_, 0:512]
            st6 = SM[0:np_, 0:6]
            mv = SM[0:np_, 8:10]
            S.op('dve', lambda e: e.bn_stats(st6, v), ins=[v], outs=[st6])
            S.op('dve', lambda e: e.bn_aggr(mv, st6), ins=[st6], outs=[mv])
            rs = SM[0:np_, 10:11]
            TS(rs, SM[0:np_, 9:10], 1e-5, None, ALU.add)
            ACT(rs, rs, AF.Sqrt)
            RECIP(rs, rs)
            TS(v, v, SM[0:np_, 8:9], rs, ALU.subtract, ALU.mult)
            TT(v, v, GLN[0][0:np_, 0:512], ALU.mult)
            if blk < 4:
                TT(VN[blk][:, 0:512], v, GLN[1][:, 0:512], ALU.add)
            else:
                TT(v, v, GLN[1][0:np_, 0:512], ALU.add)
                DMA(O['o_gv'][l], v)
        for j in range(4):
            z = T[j]
            for blk in range(4):
                ps = PS()
                MM(ps[:, 0:128], VN[blk][:, j * 128:(j + 1) * 128], LP.wmT[:, 2 * j, :])
                MM(ps[:, 128:256], VN[blk][:, j * 128:(j + 1) * 128], LP.wmT[:, 2 * j + 1, :])
                TT(z[0:64, blk * 128:(blk + 1) * 128], ps[0:64, 0:128], LP.GBt[0:64, j, :], ALU.add)
                TT(z[64:128, blk * 128:(blk + 1) * 128], ps[64:128, 128:256], LP.GBt[64:128, j, :], ALU.add)
            TT(z[:, 0:ST], z[:, 0:ST], PJ[j][:, 0:ST], ALU.mult)
            TT(oT[:, j, 0:ST], z[:, 0:ST], SG[:, j, 0:ST], ALU.mult)
        if samp:
            ps = PS()
            for j in range(4):
                TR(ps[:, j * NS:(j + 1) * NS], PJ[8][0:NS, j * 128:(j + 1) * 128], NS)
            for j in range(4):
                z = T[4][:, 0:NS]
                TS(z, ps[:, j * NS:(j + 1) * NS], LP.GS0[:, j:j + 1], LP.GS0[:, 4 + j:5 + j], ALU.mult, ALU.add)
                TT(z, z, PJs[:, j, :], ALU.mult)
                TT(oT[:, j, ST:ST + NS], z, SG[:, j, ST:ST + NS], ALU.mult)
        merge_stage(2, samp)

    AR = S.sb("AR", [128, 4, 2, 128], BF16)
    TOK = S.sb("TOK", [128, 4, 384], BF16)
    AM = S.sb("AM", [128, 8, 512], BF16)
    PQ = [S.sb("PQ%d" % i, [128, 8, 256], BF16) for i in range(2)]
    YY = [S.sb("YY%d" % i, [128, 8, 128], BF16) for i in range(2)]
    OTK = S.sb("OTK", [128, 4, 128], F32)
    RH = S.sb("RH", [128, 128], BF16)
    UU = S.sb("UU", [128, 128], BF16)
    SM2 = S.sb("SM2", [128, 32], F32)
    SO = S.sb("SO", [128, 128], F32)

    def head_norm(x3, np_, ng, sq_tile=None):
        sm = SM2[0:np_, 0:ng]
        RED(sm, x3)
        TS(sm, sm, 1.0 / 64, None, ALU.mult)
        TT(x3, x3, sm.unsqueeze(2).broadcast_to([np_, ng, 64]), ALU.subtract)
        sq = (sq_tile if sq_tile is not None else T[10])[0:np_, 0:ng * 64].rearrange("p (g k) -> p g k", g=ng)
        TT(sq, x3, x3, ALU.mult)
        vs = SM2[0:np_, 8:8 + ng]
        RED(vs, sq)
        TS(vs, vs, 1.0 / 64, 64e-5, ALU.mult, ALU.add)
        ACT(vs, vs, AF.Sqrt)
        RECIP(vs, vs)
        TT(x3, x3, vs.unsqueeze(2).broadcast_to([np_, ng, 64]), ALU.mult)

    import os as _os
    _kr = int(_os.environ.get('KR', '99'))

    def ck(n):
        return _kr <= n

    def rwkv_branch(l, st, samp):
        def shift_ops(ch):
            p = PJ[ch]
            d = T[ch % 2]
            CP(p[:, 0:1], SHC[l][:, ch:ch + 1], 'pool')
            CP(SHC[l][:, ch:ch + 1], p[:, ST:ST + 1], 'pool')
            if st == NST - 1:
                DMA(O['o_shp'][l, ch * 128:(ch + 1) * 128].rearrange("(p o) -> p o", o=1), p[:, ST:ST + 1])
            TT(d[:, 0:ST], p[:, 0:ST], p[:, 1:1 + ST], ALU.subtract)
            STT(p[:, 1:1 + ST], d[:, 0:ST], pp('mu', ch), p[:, 1:1 + ST], ALU.mult, ALU.add)

        for b in range(6):
            wb = WS.get("rw%d" % b)
            for ci in range(2):
                ch = b * 2 + ci
                ps = proj_fm(wb, ci, ST, 0)
                EV(PJ[ch][:, 1:1 + ST], ps[:, 0:ST])
                if samp:
                    ps = proj_fm(wb, ci, NS, ST)
                    EV(PJs[:, ch, :], ps[:, 0:NS])
                shift_ops(ch)
        wb = WS.get("rwlo")
        ps = proj_fm(wb, 0, ST, 0)
        EV(PJ[12][:, 1:1 + ST], ps[:, 0:ST])
        if samp:
            ps = proj_fm(wb, 0, NS, ST)
            EV(PJs[:, 12, :], ps[:, 0:NS])
        shift_ops(12)
        gate_stage('rg', samp)
        if ck(1):
            return
        if samp:
            rwkv_sample(l)
        if ck(2):
            return
        lo = TB[0]
        ACT(lo[0:64, 0:ST], PJ[12][0:64, 1:1 + ST], AF.Tanh)
        CP(lo[64:128, 0:ST], PJ[12][64:128, 1:1 + ST], 'act')
        n = ST
        yfi = {}

        def E1(j):
            r = PJ[j][:, 1:1 + n]
            k0 = PJ[4 + j][:, 1:1 + n]
            v = PJ[8 + j][:, 1:1 + n]
            sb = 16 + 12 * (j % 2)
            sg = T[1][:, 0:n]
            a = T[2][:, 0:n]
            ps = PS()
            MM(ps[:, 0:n], LP.WW[0:64, j * 128:(j + 1) * 128], lo[0:64, 0:n])
            ACT(sg, ps[:, 0:n], AF.Sigmoid, bias=pp('w0', j))
            ps = PS()
            MM(ps[:, 0:n], LP.WW[64:128, j * 128:(j + 1) * 128], lo[64:128, 0:n])
            ACT(a, ps[:, 0:n], AF.Sigmoid, bias=pp('a0', j))
            yield
            kk = T[3][:, 0:n]
            TS(kk, k0, pp('kk', j), None, ALU.mult)
            TT(T[4][:, 0:n], kk, kk, ALU.mult)
            ps = PS()
            MM(ps[:, 0:n], bones[:], T[4][:, 0:n])
            ACT(T[4][:, 0:n], ps[:, 0:n], AF.Sqrt)
            yield
            TS(T[4][:, 0:n], T[4][:, 0:n], 1e-12, None, ALU.max)
            RECIP(T[4][:, 0:n], T[4][:, 0:n])
            yield
            TT(kk, kk, T[4][:, 0:n], ALU.mult)
            TS(T[4][:, 0:n], a, pp('ka', j), LP.DER[:, 8 + j:9 + j], ALU.mult, ALU.add)
            yield
            k = T[5][:, 0:n]
            TT(k, k0, T[4][:, 0:n], ALU.mult)
            beta = T[6][:, 0:n]
            TT(beta, kk, a, ALU.mult)
            yield
            TT(T[4][:, 0:n], r, k, ALU.mult)
            TS(BRt[:, :], bones[:], pp('rk', j), None, ALU.mult)
            ps = PS()
            MM(ps[:, 0:n], BRt[:, :], T[4][:, 0:n])
            TT(k0, ps[:, 0:n], v, ALU.mult)
            yield
            cs = T[4]
            for c in range(4):
                S.op('dve', lambda e: e.tensor_tensor_scan(out=cs[:, c * 128:(c + 1) * 128], data0=ones[:, 0:128],
                                                           data1=sg[:, c * 128:(c + 1) * 128], initial=0.0,
                                                           op0=ALU.mult, op1=ALU.add),
                     ins=[ones, sg], outs=[cs])
                if c % 2 == 1:
                    yield
            cs3 = cs[:, 0:n].rearrange("p (c t) -> p c t", c=4)
            dcs = T[8][:, 0:n]
            dcs3 = dcs.rearrange("p (c t) -> p c t", c=4)
            TT(dcs3, cs3, cs3[:, :, 63:64].broadcast_to([128, 4, 128]), ALU.subtract)
            f1 = SM[:, sb:sb + 4]
            gL = SM[:, sb + 4:sb + 8]
            f2 = SM[:, sb + 8:sb + 12]
            ACT(f1, cs3[:, :, 63], AF.Exp, scale=-CDEC)
            ACT(gL, cs3[:, :, 127], AF.Exp, scale=-CDEC)
            TT(f2, cs3[:, :, 127], cs3[:, :, 63], ALU.subtract)
            ACT(f2, f2, AF.Exp, scale=-CDEC)
            yield
            E2_ = T[9][:, 0:n]
            E3 = T[10][:, 0:n]
            ACT(E2_, dcs, AF.Exp, scale=-CDEC)
            ACT(E3, dcs, AF.Exp, scale=CDEC)
            TT(dcs, dcs, sg, ALU.subtract)
            ACT(dcs, dcs, AF.Exp, scale=-CDEC)
            yield
            STT(dcs, kk, -1.0, dcs, ALU.mult, ALU.mult)
            TT(E2_, r, E2_, ALU.mult)
            yield
            TT(beta, beta, E3, ALU.mult)
            TT(k, k, E3, ALU.mult)
            yield

        def E2(j):
            v = PJ[8 + j][:, 1:1 + n]
            dcs3 = T[8][:, 0:n].rearrange("p (c t) -> p c t", c=4)
            beta = T[6][:, 0:n]
            k = T[5][:, 0:n]
            CP(AR[:, :, 0, :], dcs3, 'pool')
            CP(AR[:, :, 1, :], T[9][:, 0:n].rearrange("p (c t) -> p c t", c=4), 'pool')
            CP(TB[1][:, 0:n], beta, 'pool')
            CP(TB[2][:, 0:n], k, 'pool')
            for c in range(4):
                ps = PS()
                TR(ps[:, 0:128], v[:, c * 128:(c + 1) * 128], 128)
                TR(ps[:, 128:256], beta[:, c * 128:(c + 1) * 128], 128)
                TR(ps[:, 256:384], k[:, c * 128:(c + 1) * 128], 128)
                CP(TOK[:, c, :], ps[:, 0:384], 'act')
            for c in range(4):
                for h in range(2):
                    p_ = c * 2 + h
                    hp = slice(h * 64, (h + 1) * 64)
                    arf = AR[hp, c, :, :].rearrange("p a t -> p (a t)")
                    psX = PS()
                    MM(psX[:, 0:256], TB[1][hp, c * 128:(c + 1) * 128], arf)
                    MM(psX[:, 256:512], TB[2][hp, c * 128:(c + 1) * 128], arf)
                    TT(AM[:, p_, :], psX[:, :], mask4[:, :], ALU.mult)
                    psY = PS()
                    MM(psY[:, 0:128], AR[hp, c, 0, :], TB[1][hp, c * 128:(c + 1) * 128])
                    TT(PQ[0][:, p_, 0:128], psY[:, 0:128], mls[:, :], ALU.mult)
            CP(PQ[0][:, :, 128:256], AM[:, :, 0:128], 'pool')
            TT(YY[0][:, :, :], AM[:, :, 0:128], identb[:, :].unsqueeze(1).broadcast_to([128, 8, 128]), ALU.add)
            cur = 0
            ycur = 0

            def y_update(pq, yc):
                for p4 in range(2):
                    ps = PS()
                    for q in range(4):
                        p_ = p4 * 4 + q
                        MM(ps[:, q * 128:(q + 1) * 128], pq[:, p_, 0:128], YY[yc][:, p_, :])
                    TT(YY[1 - yc][:, p4 * 4:p4 * 4 + 4, :], ps[:, :].rearrange("p (a t) -> p a t", a=4),
                       YY[yc][:, p4 * 4:p4 * 4 + 4, :], ALU.add)

            PQ3 = [PQ[0], PQ[1]]
            for lvl in range(1, 7):
                nw = 1 - cur
                for p2 in range(4):
                    ps = PS()
                    for q in range(2):
                        p_ = p2 * 2 + q
                        MM(ps[:, q * 256:q * 256 + 128], PQ3[cur][:, p_, 128:256], PQ3[cur][:, p_, 0:128])
                        if lvl < 6:
                            MM(ps[:, q * 256 + 128:q * 256 + 256], PQ3[cur][:, p_, 0:128], PQ3[cur][:, p_, 128:256])
                    if lvl < 6:
                        CP(PQ3[nw][:, p2 * 2:p2 * 2 + 2, :], ps[:, :].rearrange("p (a t) -> p a t", a=2), 'act')
                    else:
                        CP(PQ3[nw][:, p2 * 2:p2 * 2 + 2, 0:128],
                           ps[:, :].rearrange("p (a t) -> p a t", a=2)[:, :, 0:128], 'act')
                if lvl >= 2:
                    y_update(PQ3[cur], ycur)
                    ycur = 1 - ycur
                cur = nw
            y_update(PQ3[cur], ycur)
            ycur = 1 - ycur
            cur = ycur
            yfi[j] = cur

        def Q(j):
            sb = 16 + 12 * (j % 2)
            Yf = YY[yfi[j]]
            tmp = PJ[j][:, 0:128]
            bonus = PJ[4 + j][:, 1:1 + n]
            for c in range(4):
                TS(SIGf[:, :], SIG[l][:, j, :], SM[:, sb + c:sb + c + 1], None, ALU.mult)
                ps1 = PS()
                MM(ps1[:, 0:128], AR[:, c, 0, :], SIGf[:, :], True, False)
                for h in range(2):
                    hc = slice(h * 64, (h + 1) * 64)
                    MM(ps1[:, hc], AM[:, c * 2 + h, 256:384], TOK[:, c, hc], False, h == 1)
                CP(RH[:, :], ps1[:, 0:128], 'act')
                yield
                ps2 = PS()
                for h in range(2):
                    hc = slice(h * 64, (h + 1) * 64)
                    MM(ps2[:, hc], Yf[:, c * 2 + h, :], RH[:, hc])
                CP(UU[:, :], ps2[:, 0:128], 'act')
                yield
                ps3 = PS()
                MM(ps3[:, 0:128], AR[:, c, 1, :], SIGf[:, :], True, False)
                for h in range(2):
                    hc = slice(h * 64, (h + 1) * 64)
                    MM(ps3[:, hc], AM[:, c * 2 + h, 128:256], UU[:, hc], False, False)
                    MM(ps3[:, hc], AM[:, c * 2 + h, 384:512], TOK[:, c, hc], False, h == 1)
                CP(OTK[:, c, :], ps3[:, 0:128], 'act')
                ps4 = PS()
                MM(ps4[:, 0:128], TOK[:, c, 128:256], UU[:, :], True, False)
                MM(ps4[:, 0:128], TOK[:, c, 256:384], TOK[:, c, 0:128], False, True)
                STT(tmp, ps4[:, 0:128], SM[:, sb + 8 + c:sb + 9 + c], bones[:], ALU.mult, ALU.mult)
                STT(SIG[l][:, j, :], SIG[l][:, j, :], SM[:, sb + 4 + c:sb + 5 + c], tmp, ALU.mult, ALU.add)
                yield
            head_norm(OTK[:, :, :].rearrange("p c (h v) -> p (c h) v", h=2), 128, 8, sq_tile=PJ[j])
            yield
            ps = PS()
            for c in range(4):
                TR(ps[:, c * 128:(c + 1) * 128], OTK[:, c, :], 128)
            ob = PJ[8 + j][:, 1:1 + n]
            ACT(ob, ps[:, 0:n], AF.Identity, bias=pp('lxb', j), scale=pp('lxg', j))
            yield
            TT(ob, ob, bonus, ALU.add)
            TT(oT[:, j, 0:n], ob, SG[:, j, 0:n], ALU.mult)
            if st == NST - 1:
                ps = PS()
                TR(ps[:, 0:128], SIG[l][:, j, :], 128)
                CP(SO[:, :], ps[:, 0:128], 'act')
                for h in range(2):
                    DMA(O['o_Sp'][l, 2 * j + h], SO[h * 64:(h + 1) * 64, h * 64:(h + 1) * 64])
            yield

        for _ in E1(0):
            pass
        E2(0)
        for j in range(4):
            gq = Q(j)
            ge = E1(j + 1) if j < 3 else None
            alive_q, alive_e = True, ge is not None
            while alive_q or alive_e:
                if alive_q:
                    try:
                        next(gq)
                    except StopIteration:
                        alive_q = False
                if alive_e:
                    try:
                        next(ge)
                    except StopIteration:
                        alive_e = False
            if j < 3:
                E2(j + 1)
        merge_stage(1, samp)

    _ks = int(_os.environ.get('KS', '99'))

    def rwkv_sample(l):
        n = NS
        DMA(T[0][0:NS, 0:512], I['st_sh'][l][:, 0:512])
        DMA(T[1][0:NS, 0:512], I['st_sh'][l][:, 512:1024])
        DMA(T[2][0:NS, 0:512], I['st_sh'][l][:, 1024:1536])
        DMA(T[3][0:NS, 0:128], I['st_sh'][l][:, 1536:1664])
        PV = T[4]
        PVv = PV[:, 0:13 * NS].rearrange("p (c s) -> p c s", c=13)
        ps = PS()
        for ch in range(13):
            src = T[ch // 4][0:NS, (ch % 4) * 128:(ch % 4 + 1) * 128]
            TR(ps[:, ch * NS:(ch + 1) * NS], src, NS)
        CP(PV[:, 0:13 * NS], ps[:, 0:13 * NS], 'act')
        for g in range(4):
            nch = 4 if g < 3 else 1
            ps = PS()
            for q in range(nch):
                TR(ps[0:NS, q * 128:(q + 1) * 128], PJs[:, g * 4 + q, :], 128)
            CP(T[5][0:NS, 0:nch * 128], ps[0:NS, 0:nch * 128], 'act')
            DMA(O['o_shs'][l][:, g * 512:g * 512 + nch * 128], T[5][0:NS, 0:nch * 128])
        if _ks <= 1:
            return
        XSv = T[6][:, 0:13 * NS].rearrange("p (c s) -> p c s", c=13)
        for ch in range(13):
            TT(T[7][:, 0:n], PVv[:, ch, :], PJs[:, ch, :], ALU.subtract)
            STT(XSv[:, ch, :], T[7][:, 0:n], pp('mu', ch), PJs[:, ch, :], ALU.mult, ALU.add)
        lo = TB[0]
        ACT(lo[0:64, 0:n], XSv[0:64, 12, :], AF.Tanh)
        CP(lo[64:128, 0:n], XSv[64:128, 12, :], 'act')
        STG = T[8][:, 0:6 * 4 * NS].rearrange("p (o j s) -> p o j s", o=6, j=4)
        BON = T[9][:, 0:4 * NS].rearrange("p (j s) -> p j s", j=4)
        for j in range(4):
            r = XSv[:, j, :]
            k0 = XSv[:, 4 + j, :]
            v = XSv[:, 8 + j, :]
            sg = T[1][:, 0:n]
            a = T[2][:, 0:n]
            ps = PS()
            MM(ps[:, 0:n], LP.WW[0:64, j * 128:(j + 1) * 128], lo[0:64, 0:n])
            ACT(sg, ps[:, 0:n], AF.Sigmoid, bias=pp('w0', j))
            ps = PS()
            MM(ps[:, 0:n], LP.WW[64:128, j * 128:(j + 1) * 128], lo[64:128, 0:n])
            ACT(a, ps[:, 0:n], AF.Sigmoid, bias=pp('a0', j))
            kk = T[3][:, 0:n]
            TS(kk, k0, pp('kk', j), None, ALU.mult)
            ACT(T[5][:, 0:n], kk, AF.Square)
            ps = PS()
            MM(ps[:, 0:n], bones[:], T[5][:, 0:n])
            ACT(T[5][:, 0:n], ps[:, 0:n], AF.Sqrt)
            TS(T[5][:, 0:n], T[5][:, 0:n], 1e-12, None, ALU.max)
            RECIP(T[5][:, 0:n], T[5][:, 0:n])
            TT(kk, kk, T[5][:, 0:n], ALU.mult)
            TS(T[5][:, 0:n], a, pp('ka', j), LP.DER[:, 8 + j:9 + j], ALU.mult, ALU.add)
            TT(STG[:, 2, j, :], k0, T[5][:, 0:n], ALU.mult)
            CP(STG[:, 0, j, :], r, 'pool')
            ACT(STG[:, 1, j, :], sg, AF.Exp, scale=-CDEC)
            CP(STG[:, 3, j, :], v, 'pool')
            TS(STG[:, 4, j, :], kk, -1.0, None, ALU.mult)
            TT(STG[:, 5, j, :], kk, a, ALU.mult)
            TT(T[5][:, 0:n], r, STG[:, 2, j, :], ALU.mult)
            TS(BRt[:, :], bones[:], pp('rk', j), None, ALU.mult)
            ps = PS()
            MM(ps[:, 0:n], BRt[:, :], T[5][:, 0:n])
            TT(BON[:, j, :], ps[:, 0:n], v, ALU.mult)
        if _ks <= 2:
            return
        ROWS = T[10]
        for o in range(6):
            ps = PS()
            for j in range(4):
                TR(ps[0:NS, j * 128:(j + 1) * 128], STG[:, o, j, :], 128)
            CP(ROWS[0:NS, 0:512], ps[0:NS, 0:512], 'act')
            S.dma('sp', scr_in[o], ROWS[0:NS, 0:512], ins=[ROWS], outs=['scr_in'])
        OPS = T[0][:, 0:384].rearrange("p (o k) -> p o k", o=6)
        S.dma('sp', OPS, scr_in.rearrange("o s (h k) -> (s h) o k", h=8), ins=['scr_in'], outs=[T[0]])
        if _ks <= 3:
            return
        rr, ww_, kk_, vv, nk, ka_ = [OPS[:, o, :] for o in range(6)]
        osh = T[1][:, 0:64]
        Sv = I['st_S'][l].rearrange("s h v k -> (s h) v k")
        So = O['o_Ss'][l].rearrange("s h v k -> (s h) v k")
        for q in range(4):
            S0 = T[2][:, 0:512].rearrange("p (v k) -> p v k", v=8)
            for half in range(2):
                vq = q * 16 + half * 8
                S0 = T[2 + half][:, 0:512].rearrange("p (v k) -> p v k", v=8)
                tmp = T[4 + half][:, 0:512].rearrange("p (v k) -> p v k", v=8)
                DMA(S0, Sv[:, vq:vq + 8, :])
                bk = lambda x: x.unsqueeze(1).broadcast_to([128, 8, 64])
                TT(tmp, S0, bk(nk), ALU.mult)
                sa = SM2[:, 16 + half * 8:24 + half * 8]
                RED(sa, tmp)
                TT(S0, S0, bk(ww_), ALU.mult)
                TT(tmp, sa.unsqueeze(2).broadcast_to([128, 8, 64]), bk(ka_), ALU.mult)
                TT(S0, S0, tmp, ALU.add)
                TT(tmp, vv[:, vq:vq + 8].unsqueeze(2).broadcast_to([128, 8, 64]), bk(kk_), ALU.mult)
                TT(S0, S0, tmp, ALU.add)
                TT(tmp, S0, bk(rr), ALU.mult)
                RED(osh[:, vq:vq + 8], tmp)
                DMA(So[:, vq:vq + 8, :], S0)
        if _ks <= 4:
            return
        S.dma('sp', scr_o, osh, ins=[T[1]], outs=['scr_o'])
        OR = T[2][0:NS, 0:512]
        S.dma('sp', OR, scr_o.rearrange("(s h) v -> s (h v)", h=8), ins=['scr_o'], outs=[T[2]])
        if _ks <= 5:
            return
        head_norm(OR.rearrange("s (h v) -> s h v", h=8), NS, 8)
        if _ks <= 6:
            return
        ps = PS()
        for j in range(4):
            TR(ps[:, j * NS:(j + 1) * NS], T[2][0:NS, j * 128:(j + 1) * 128], NS)
        for j in range(4):
            ob = T[3][:, 0:n]
            ACT(ob, ps[:, j * NS:(j + 1) * NS], AF.Identity, bias=pp('lxb', j), scale=pp('lxg', j))
            TT(ob, ob, BON[:, j, :], ALU.add)
            TT(oT[:, j, ST:ST + n], ob, SG[:, j, ST:ST + n], ALU.mult)

    def out_stage(l, st, samp):
        GB = [T[0], T[1]]
        LN = [[PJ[2 * k_ + h_] for h_ in range(2)] for k_ in range(3)]
        for k_, nm in enumerate(['b_out', 'ln_g', 'ln_b']):
            for h_ in range(2):
                DMA(LN[k_][h_][:, 0:512], I[nm][l:l + 1, h_ * 512:(h_ + 1) * 512].broadcast_to([128, 512]))
        for hh in range(2):
            ps = PS()
            MM(ps[:, 0:512], ones[32:33, 0:128], G33[l][32:33, hh * 512:(hh + 1) * 512])
            CP(GB[hh][:, 0:512], ps[:, 0:512], 'act')
        def final_ln(blk):
            np_ = 128 if blk < 4 else NS
            xv = Xb[blk][:, :] if blk < 4 else XS[:, :]
            st6 = SM[0:np_, 32:44]
            mv = SM[0:np_, 44:46]
            S.op('dve', lambda e: e.bn_stats(st6[:, 0:6], xv[:, 0:512]), ins=[xv], outs=[st6])
            S.op('dve', lambda e: e.bn_stats(st6[:, 6:12], xv[:, 512:1024]), ins=[xv], outs=[st6])
            S.op('dve', lambda e: e.bn_aggr(mv, st6), ins=[st6], outs=[mv])
            rs = SM[0:np_, 46:47]
            TS(rs, SM[0:np_, 45:46], 1e-5, None, ALU.add)
            ACT(rs, rs, AF.Sqrt)
            RECIP(rs, rs)
            TS(xv, xv, SM[0:np_, 44:45], rs, ALU.subtract, ALU.mult)
            for h_ in range(2):
                TT(xv[:, h_ * 512:(h_ + 1) * 512], xv[:, h_ * 512:(h_ + 1) * 512], LN[1][h_][0:np_, 0:512], ALU.mult)
                TT(xv[:, h_ * 512:(h_ + 1) * 512], xv[:, h_ * 512:(h_ + 1) * 512], LN[2][h_][0:np_, 0:512], ALU.add)
            if l == DEPTH - 1:
                if blk < 4:
                    DMA(O['o_yp'][st * ST + blk * 128: st * ST + (blk + 1) * 128, :], xv)
                else:
                    DMA(O['o_ys'], xv)

        for i in range(4):
            wo = WS.get("wo%d" % i)
            c0 = i * 256
            for blk in range(5 if samp else 4):
                np_ = 128 if blk < 4 else NS
                ps = PS()
                for kc in range(8):
                    lhs = mT[:, kc, blk * 128:(blk + 1) * 128] if blk < 4 else mT[:, kc, ST:ST + NS]
                    MM(ps[0:np_, 0:256], lhs, wo[:, kc, :], kc == 0, kc == 7)
                t = T[2][0:np_, 0:256]
                TT(t, ps[0:np_, 0:256], LN[0][c0 // 512][0:np_, (c0 % 512):(c0 % 512) + 256], ALU.add)
                if blk < 4:
                    TT(t, t, GB[c0 // 512][:, (c0 % 512):(c0 % 512) + 256], ALU.mult)
                    xv = Xb[blk][:, c0:c0 + 256]
                else:
                    TT(t, t, G33[l][0:NS, c0:c0 + 256], ALU.mult)
                    xv = XS[:, c0:c0 + 256]
                STT(xv, xv, ALPHA, t, ALU.mult, ALU.add)
                if i == 3:
                    final_ln(blk)

    import os
    stop = int(os.environ.get('KSTOP', '999'))

    def run_all():
        k = 0
        if stop <= k:
            return
        cond_setup()
        for l in range(DEPTH):
            load_params(l)
        for st in range(NST):
            for l in range(DEPTH):
                samp = (st == 0)
                stages = [lambda: set_layer(l)]
                if st == 0:
                    stages.append(lambda: cond_layer(l))
                stages += [lambda: build_hT(l, st, samp), lambda: lru_branch(l, st, samp),
                           lambda: rwkv_branch(l, st, samp), lambda: gmlp_branch(l, st, samp),
                           lambda: conf_branch(l, st, samp), lambda: out_stage(l, st, samp)]
                for f in stages:
                    k += 1
                    if stop <= k:
                        return
                    f()

    run_all()
    S.finish()
    return nc


_NC_CACHE = {}


def kernel(**inputs):
    inp = {k: np.ascontiguousarray(np.asarray(v, dtype=np.float32)) for k, v in inputs.items()}
    if 'nc' not in _NC_CACHE:
        _NC_CACHE['nc'] = build_nc()
    nc = _NC_CACHE['nc']
    in_maps = []
    for b in range(8):
        sl = slice(b * NS, (b + 1) * NS)
        m = {n: inp[n] for n, _ in WEIGHT_NAMES}
        m['xp'] = inp['x_prompt'][b]
        m['xs'] = np.ascontiguousarray(inp['x_sample'][sl, 0, :])
        c33 = np.zeros((NSC, D), np.float32)
        c33[0:NS] = inp['c_sample'][sl]
        c33[32] = inp['c_prompt'][b]
        m['c33'] = c33
        m['st_lc'] = np.ascontiguousarray(inp['state_lru_conv'][:, sl])
        m['st_lh'] = np.ascontiguousarray(inp['state_lru_h'][:, sl])
        m['st_sh'] = np.ascontiguousarray(inp['state_rwkv_shift'][:, sl])
        m['st_S'] = np.ascontiguousarray(inp['state_rwkv_S'][:, sl])
        m['st_cf'] = np.ascontiguousarray(inp['state_conf_conv'][:, sl])
        in_maps.append(m)
    res = run_bass_kernel_spmd(nc, in_maps, core_ids=list(range(8)))
    R = res.results
    y_p = np.stack([R[b]['o_yp'] for b in range(8)], 0)
    y_s = np.concatenate([R[b]['o_ys'] for b in range(8)], 0)[:, None, :]

    def cat_p(n):
        return np.stack([R[b][n] for b in range(8)], 1)

    def cat_s(n):
        return np.concatenate([R[b][n] for b in range(8)], 1)

    outs = (y_p, y_s, cat_p('o_lcp'), cat_s('o_lcs'), cat_p('o_lhp'), cat_s('o_lhs'),
            cat_p('o_shp'), cat_s('o_shs'), cat_p('o_Sp'), cat_s('o_Ss'),
            cat_p('o_cfp'), cat_s('o_cfs'), cat_s('o_gv')[:, :, None, :])
    return tuple(np.ascontiguousarray(o.astype(np.float32)) for o in outs)
```

```python
import contextlib
import numpy as np
import concourse.bass as bass
import concourse.mybir as mybir
from concourse.bass_utils import run_bass_kernel_spmd
from concourse.alu_op_type import AluOpType as ALU

F32 = mybir.dt.float32
BF16 = mybir.dt.bfloat16
AF = mybir.ActivationFunctionType
AX = mybir.AxisListType

N_DSEM = 24
SEQ = 2048
ST = 512
NST = SEQ // ST
NS = 16
NSC = 33
D = 1024
BW = 512
DEPTH = 2
SHW = 1664
CDEC = 0.606531
ALPHA = (2.0 * DEPTH) ** 0.25
OFF = dict(LRU_X=0, LRU_G=512, RWKV=1024, RWKV_G=2688, GM_U=3200, GM_V=3712, GM_G=4224,
           CF_A=4736, CF_B=5248, CF_G=5760, MERGE=6272)
WCOL = 544


def _key(x):
    if isinstance(x, str):
        return x
    t = getattr(x, 'tensor', None)
    if t is not None:
        return t.name
    return x.name


class Sched:
    def __init__(self, nc):
        self.nc = nc
        self.stack = contextlib.ExitStack()
        self.eng = {'pe': nc.tensor, 'act': nc.scalar, 'dve': nc.vector,
                    'pool': nc.gpsimd, 'sp': nc.sync}
        self.sem = {e: self.stack.enter_context(nc.semaphore("s_" + e)) for e in self.eng}
        self.cnt = {e: 0 for e in self.eng}
        self.dsem = [self.stack.enter_context(nc.semaphore("d%d" % i)) for i in range(N_DSEM)]
        self.dcnt = [0] * N_DSEM
        self.dnext = 0
        self.waited = {e: {} for e in self.eng}
        self.bufs = {}
        self.n_ops = 0

    def sb(self, name, shape, dt):
        return self.stack.enter_context(self.nc.sbuf_tensor(name, list(shape), dt))

    def ps(self, name, shape, dt):
        return self.stack.enter_context(self.nc.psum_tensor(name, list(shape), dt))

    def _need(self, e, r, w):
        need = []
        for k in r:
            b = self.bufs.get(k)
            if b and b['w'] is not None:
                need.append(b['w'])
        for k in w:
            b = self.bufs.get(k)
            if b:
                if b['w'] is not None:
                    need.append(b['w'])
                need.extend(b['r'])
        en = self.eng[e]
        for t in need:
            kind, idx, val = t
            if kind == 'eng' and idx == 'pe' and e == 'pe':
                continue
            key = (kind, idx)
            if self.waited[e].get(key, 0) >= val:
                continue
            self.waited[e][key] = val
            s = self.sem[idx] if kind == 'eng' else self.dsem[idx]
            en.wait_ge(s, val)

    def _record(self, ticket, r, w):
        for k in w:
            self.bufs[k] = {'w': ticket, 'r': []}
        for k in r:
            if k in w:
                continue
            b = self.bufs.setdefault(k, {'w': None, 'r': []})
            b['r'] = [t for t in b['r'] if not (t[0] == ticket[0] and t[1] == ticket[1])]
            b['r'].append(ticket)

    def op(self, e, fn, ins=(), outs=()):
        r = [_key(x) for x in ins]
        w = [_key(x) for x in outs]
        self._need(e, r, w)
        ins_ = fn(self.eng[e])
        self.cnt[e] += 1
        ins_.then_inc(self.sem[e], 1)
        self._record(('eng', e, self.cnt[e]), r, w)
        self.n_ops += 1
        return ins_

    def dma(self, q, out, in_, ins=(), outs=(), **kw):
        r = [_key(x) for x in ins]
        w = [_key(x) for x in outs]
        i = self.dnext
        self.dnext = (self.dnext + 1) % N_DSEM
        en = self.eng[q]
        if self.dcnt[i] > 0 and self.waited[q].get(('dma', i), 0) < self.dcnt[i]:
            self.waited[q][('dma', i)] = self.dcnt[i]
            en.wait_ge(self.dsem[i], self.dcnt[i])
        self._need(q, r, w)
        self.dcnt[i] += 16
        en.dma_start(out=out, in_=in_, **kw).then_inc(self.dsem[i], 16)
        t = ('dma', i, self.dcnt[i])
        self._record(t, r, w)
        return t

    def finish(self):
        en = self.eng['sp']
        for i in range(N_DSEM):
            if self.dcnt[i] > 0:
                en.wait_ge(self.dsem[i], self.dcnt[i])
        for e in self.eng:
            if e != 'sp' and self.cnt[e] > 0:
                en.wait_ge(self.sem[e], self.cnt[e])
        self.stack.close()


WEIGHT_NAMES = [
    ('w_cond', (2, 1024, 3072)), ('b_cond', (2, 3072)), ('w_in', (2, 1024, 10368)),
    ('lru_conv_w', (2, 4, 512)), ('lru_conv_b', (2, 512)), ('lru_wr', (2, 8, 64, 64)),
    ('lru_br', (2, 512)), ('lru_wi', (2, 8, 64, 64)), ('lru_bi', (2, 512)), ('lru_lambda', (2, 512)),
    ('rwkv_mu', (2, 1664)), ('rwkv_w0', (2, 512)), ('rwkv_ww', (2, 64, 512)), ('rwkv_a0', (2, 512)),
    ('rwkv_wa', (2, 64, 512)), ('rwkv_kk', (2, 512)), ('rwkv_ka', (2, 512)), ('rwkv_rk', (2, 8, 64)),
    ('rwkv_lnx_g', (2, 512)), ('rwkv_lnx_b', (2, 512)), ('gmlp_ln_g', (2, 512)), ('gmlp_ln_b', (2, 512)),
    ('gmlp_ws', (2, 8, 128, 128)), ('gmlp_bs', (2, 8, 128)), ('conf_dw_w', (2, 31, 512)),
    ('conf_dw_b', (2, 512)), ('conf_ln_g', (2, 512)), ('conf_ln_b', (2, 512)),
    ('w_branch', (2, 4, 512, 1024)), ('w_out', (2, 1024, 1024)), ('b_out', (2, 1024)),
    ('ln_g', (2, 1024)), ('ln_b', (2, 1024)),
]
STATE_NAMES = [('xp', (SEQ, D)), ('xs', (NS, D)), ('c33', (NSC, D)),
               ('st_lc', (2, NS, 3, 512)), ('st_lh', (2, NS, 512)), ('st_sh', (2, NS, SHW)),
               ('st_S', (2, NS, 8, 64, 64)), ('st_cf', (2, NS, 30, 512))]
OUT_NAMES = [('o_yp', (SEQ, D)), ('o_ys', (NS, D)), ('o_lcp', (2, 3, 512)), ('o_lcs', (2, NS, 3, 512)),
             ('o_lhp', (2, 512)), ('o_lhs', (2, NS, 512)), ('o_shp', (2, SHW)), ('o_shs', (2, NS, SHW)),
             ('o_Sp', (2, 8, 64, 64)), ('o_Ss', (2, NS, 8, 64, 64)), ('o_cfp', (2, 30, 512)),
             ('o_cfs', (2, NS, 30, 512)), ('o_gv', (2, NS, 512))]


def build_nc():
    nc = bass.Bass("TRN2", target_bir_lowering=False)
    S = Sched(nc)
    I = {}
    for n, shp in WEIGHT_NAMES + STATE_NAMES:
        I[n] = nc.dram_tensor(n, list(shp), F32, kind="ExternalInput").ap()
    O = {}
    for n, shp in OUT_NAMES:
        O[n] = nc.dram_tensor(n, list(shp), F32, kind="ExternalOutput").ap()
    scr_in = nc.dram_tensor("scr_in", [6, NS, 512], F32, kind="Internal").ap()
    scr_o = nc.dram_tensor("scr_o", [NS * 8, 64], F32, kind="Internal").ap()

    def _aps(*xs):
        return [x for x in xs if x is not None and not isinstance(x, (int, float))]

    def ACT(out, in_, func, bias=None, scale=None):
        kw = {}
        if bias is not None:
            kw['bias'] = bias
        if scale is not None:
            kw['scale'] = scale
        S.op('act', lambda e: e.activation(out=out, in_=in_, func=func, **kw),
             ins=_aps(in_, bias, scale), outs=[out])

    def TT(out, a, b, op, eng='dve'):
        S.op(eng, lambda e: e.tensor_tensor(out=out, in0=a, in1=b, op=op), ins=[a, b], outs=[out])

    def TS(out, a, s1, s2, op0, op1=None, eng='dve'):
        if op1 is None:
            S.op(eng, lambda e: e.tensor_scalar(out, a, s1, None, op0), ins=_aps(a, s1), outs=[out])
        else:
            S.op(eng, lambda e: e.tensor_scalar(out, a, s1, s2, op0, op1), ins=_aps(a, s1, s2), outs=[out])

    def STT(out, a, s, b, op0, op1):
        S.op('dve', lambda e: e.scalar_tensor_tensor(out=out, in0=a, scalar=s, in1=b, op0=op0, op1=op1),
             ins=_aps(a, s, b), outs=[out])

    def CP(out, in_, eng='dve'):
        if eng == 'act':
            S.op('act', lambda e: e.copy(out, in_), ins=[in_], outs=[out])
        else:
            S.op(eng, lambda e: e.tensor_copy(out, in_), ins=[in_], outs=[out])

    def MM(out, lhsT, rhs, start=True, stop=True):
        S.op('pe', lambda e: e.matmul(out, lhsT, rhs, start=start, stop=stop), ins=[lhsT, rhs], outs=[out])

    def TR(out, in_, n):
        S.op('pe', lambda e: e.transpose(out, in_, ident[0:n, 0:n]), ins=[in_, ident], outs=[out])

    def RED(out, in_, op=ALU.add):
        S.op('dve', lambda e: e.tensor_reduce(out=out, in_=in_, axis=AX.X, op=op), ins=[in_], outs=[out])

    def RECIP(out, in_):
        S.op('dve', lambda e: e.reciprocal(out, in_), ins=[in_], outs=[out])

    def MEMSET(ap, v, eng='pool'):
        S.op(eng, lambda e: e.memset(ap, v), ins=[], outs=[ap])

    def DMA(out, in_, q=None, **kw):
        in_dram = in_.tensor.name in D_names
        out_dram = out.tensor.name in D_names
        ins = [] if in_dram else [in_]
        outs = [] if out_dram else [out]
        if q is None:
            q = 'pool' if out_dram else 'sp'
        S.dma(q, out, in_, ins=ins, outs=outs, **kw)

    D_names = set(I.keys()) | set(O.keys())

    psum = [S.ps("pb%d" % i, [128, 512], F32) for i in range(8)]
    pstate = {'i': 0}

    def PS():
        t = psum[pstate['i'] % 8]
        pstate['i'] += 1
        return t

    ident = S.sb("ident", [128, 128], F32)
    identb = S.sb("identb", [128, 128], BF16)
    ones = S.sb("ones", [128, 128], F32)
    bones = S.sb("bones", [128, 128], F32)
    mask4 = S.sb("mask4", [128, 512], BF16)
    mls = S.sb("mls", [128, 128], BF16)

    def SEL(out, in_, pattern, cm, op, fill=0.0):
        S.op('pool', lambda e: e.affine_select(out=out, in_=in_, pattern=pattern, compare_op=op, fill=fill,
                                               base=0, channel_multiplier=cm), ins=[in_], outs=[out])

    Xb = [S.sb("X%d" % i, [128, D], F32) for i in range(4)]
    XS = S.sb("XS", [NS, D], F32)
    hT = S.sb("hT", [128, 8, 528], BF16)
    mT = S.sb("mT", [128, 8, 528], BF16)
    oT = S.sb("oT", [128, 4, 528], BF16)
    SG = S.sb("SG", [128, 4, 528], BF16)
    PJ = [S.sb("PJ%d" % i, [128, WCOL], F32) for i in range(13)]
    PJs = S.sb("PJs", [128, 13, NS], F32)
    T = [S.sb("T%d" % i, [128, WCOL], F32) for i in range(11)]
    TB = [S.sb("TB%d" % i, [128, WCOL], BF16) for i in range(4)]
    SM = S.sb("SM", [128, 64], F32)
    mtmp = T[0][:, 0:128]
    MEMSET(ident[:], 0.0)
    SEL(ident[:], ident[:], [[-1, 128]], 1, ALU.not_equal, 1.0)
    CP(identb[:], ident[:])
    MEMSET(ones[:], 1.0)
    MEMSET(bones[:], 0.0)
    MEMSET(bones[0:64, 0:64], 1.0)
    MEMSET(bones[64:128, 64:128], 1.0)
    MEMSET(mtmp, 1.0)
    SEL(mtmp, mtmp, [[1, 128]], -1, ALU.is_gt)
    CP(mask4[:, 0:128], mtmp)
    CP(mask4[:, 256:384], mtmp)
    MEMSET(mtmp, 1.0)
    SEL(mtmp, mtmp, [[1, 128]], -1, ALU.is_ge)
    CP(mask4[:, 128:256], mtmp)
    CP(mask4[:, 384:512], mtmp)
    MEMSET(mtmp, 1.0)
    SEL(mtmp, mtmp, [[-1, 128]], 1, ALU.is_gt)
    CP(mls[:], mtmp)

    SCT = S.sb("SCT", [128, 8, NSC], BF16)
    SCPR = S.sb("SCPR", [128, 8, 128], BF16)
    modT = [S.sb("modT%d" % l, [128, 16, NSC], F32) for l in range(2)]
    G33 = [S.sb("G33_%d" % l, [NSC, D], F32) for l in range(2)]
    LCH = [S.sb("LCH%d" % l, [128, 4, 3], F32) for l in range(2)]
    LH = [S.sb("LH%d" % l, [128, 4], F32) for l in range(2)]
    SHC = [S.sb("SHC%d" % l, [128, 13], F32) for l in range(2)]
    SIG = [S.sb("SIG%d" % l, [128, 4, 128], F32) for l in range(2)]
    CGC = [S.sb("CGC%d" % l, [128, 4, 30], F32) for l in range(2)]
    SIGf = S.sb("SIGf", [128, 128], BF16)
    for l in range(2):
        MEMSET(LCH[l][:], 0.0)
        MEMSET(LH[l][:], 0.0)
        MEMSET(SHC[l][:], 0.0)
        MEMSET(SIG[l][:], 0.0)
        MEMSET(CGC[l][:], 0.0)

    PST = S.sb("PST", [128, 128], F32)
    CST = S.sb("CST", [124, 128], F32)
    BRt = S.sb("BRt", [128, 128], F32)

    class _Cur:
        pass

    LP = _Cur()
    LPAR = []
    for l_ in range(2):
        d_ = dict(PP=S.sb("PP%d" % l_, [128, 128], F32), CW=S.sb("CW%d" % l_, [128, 124], F32),
                  WRI=S.sb("WRI%d" % l_, [128, 8, 128], BF16), WW=S.sb("WW%d" % l_, [128, 512], BF16),
                  wmT=S.sb("wmT%d" % l_, [128, 8, 128], BF16), GBt=S.sb("GBt%d" % l_, [128, 4, 128], F32),
                  GS0=S.sb("GS0%d" % l_, [128, 8], F32), DER=S.sb("DER%d" % l_, [128, 16], F32))
        LPAR.append(d_)

    def set_layer(l):
        for k_, v_ in LPAR[l].items():
            setattr(LP, k_, v_)

    PPC = {}
    _r = 0
    for nm, rows in [('lcw', 16), ('lcb', 4), ('lbr', 4), ('lbi', 4), ('llam', 4), ('mu', 13), ('w0', 4),
                     ('a0', 4), ('kk', 4), ('ka', 4), ('rk', 4), ('lxg', 4), ('lxb', 4), ('cfb', 4),
                     ('cfg', 4), ('cfbb', 4), ('bc', 16)]:
        PPC[nm] = _r
        _r += rows
    assert _r <= 128

    def pp(nm, i=0):
        c = PPC[nm] + i
        return LP.PP[:, c:c + 1]

    def rows128(ap1d):
        return ap1d.rearrange("(j p) -> j p", p=128)

    def load_params(l):
        set_layer(l)
        W = I
        MEMSET(PST[:], 0.0)
        for nm, src in [('lcw', W['lru_conv_w'][l].rearrange("k (j p) -> (k j) p", p=128)),
                        ('lcb', rows128(W['lru_conv_b'][l])), ('lbr', rows128(W['lru_br'][l])),
                        ('lbi', rows128(W['lru_bi'][l])), ('llam', rows128(W['lru_lambda'][l])),
                        ('mu', rows128(W['rwkv_mu'][l])), ('w0', rows128(W['rwkv_w0'][l])),
                        ('a0', rows128(W['rwkv_a0'][l])), ('kk', rows128(W['rwkv_kk'][l])),
                        ('ka', rows128(W['rwkv_ka'][l])),
                        ('rk', W['rwkv_rk'][l].rearrange("(j a) k -> j (a k)", a=2)),
                        ('lxg', rows128(W['rwkv_lnx_g'][l])), ('lxb', rows128(W['rwkv_lnx_b'][l])),
                        ('cfb', rows128(W['conf_dw_b'][l])), ('cfg', rows128(W['conf_ln_g'][l])),
                        ('cfbb', rows128(W['conf_ln_b'][l])), ('bc', rows128(W['b_cond'][l, 0:2048]))]:
            n = src.shape[0]
            DMA(PST[PPC[nm]:PPC[nm] + n, :], src)
        ps = PS()
        TR(ps[:, 0:128], PST[:, :], 128)
        CP(LP.PP[:, :], ps[:, 0:128], 'act')
        for j in range(4):
            DMA(CST[j * 31:(j + 1) * 31, :], W['conf_dw_w'][l][:, j * 128:(j + 1) * 128])
        ps = PS()
        TR(ps[:, 0:124], CST[:, :], 124)
        CP(LP.CW[:, :], ps[:, 0:124], 'act')
        ACT(LP.DER[:, 0:4], LP.PP[:, PPC['llam']:PPC['llam'] + 4], AF.Exp, scale=-1.0)
        ACT(LP.DER[:, 0:4], LP.DER[:, 0:4], AF.Ln, bias=1.0)
        TS(LP.DER[:, 4:8], LP.DER[:, 0:4], -16.0, None, ALU.mult)
        TS(LP.DER[:, 0:4], LP.DER[:, 0:4], -8.0, None, ALU.mult)
        TS(LP.DER[:, 8:12], LP.PP[:, PPC['ka']:PPC['ka'] + 4], -1.0, 1.0, ALU.mult, ALU.add)
        for g, nm in enumerate(['lru_wr', 'lru_wi']):
            stg = T[1 + g][:, 0:512].rearrange("p (j o) -> p j o", j=4)
            MEMSET(T[1 + g][:, 0:512], 0.0)
            for h in range(2):
                src = W[nm][l].rearrange("(j a) i o -> a i j o", a=2)[h]
                DMA(stg[h * 64:(h + 1) * 64, :, h * 64:(h + 1) * 64], src)
            CP(LP.WRI[:, g * 4:(g + 1) * 4, :], stg)
        DMA(T[3][0:64, 0:512], W['rwkv_ww'][l])
        DMA(T[3][64:128, 0:512], W['rwkv_wa'][l])
        CP(LP.WW[:], T[3][:, 0:512])
        for g in range(8):
            DMA(T[0][:, 0:128], W['gmlp_ws'][l, g])
            ps = PS()
            TR(ps[:, 0:128], T[0][:, 0:128], 128)
            TT(LP.wmT[:, g, :], ps[:, 0:128], mask4[:, 128:256], ALU.mult)
        for h in range(2):
            src = W['gmlp_bs'][l].rearrange("(j a) t -> a j t", a=2)[h]
            DMA(LP.GBt[h * 64:(h + 1) * 64, :, :], src.partition_broadcast(64))
            s2 = W['gmlp_ws'][l].rearrange("(j a) t s -> a j (t s)", a=2)[h][:, 0]
            DMA(LP.GS0[h * 64:(h + 1) * 64, 0:4], s2.partition_broadcast(64), allow_slow_non_contiguous=True)
            s3 = W['gmlp_bs'][l].rearrange("(j a) t -> a j t", a=2)[h][:, 0]
            DMA(LP.GS0[h * 64:(h + 1) * 64, 4:8], s3.partition_broadcast(64), allow_slow_non_contiguous=True)

    NSTG, NSLOT = 2, 5
    NBLK_MAX = 140
    wstg = [S.sb("wstg%d" % i, [128, 2048], F32) for i in range(NSTG)]
    wslot = [S.sb("wslot%d" % i, [128, 2048], BF16) for i in range(NSLOT)]
    wscr = [nc.dram_tensor("wscr%d" % l, [NBLK_MAX, 128, 2048], BF16, kind="Internal").ap() for l in range(2)]
    cast_rr = ['act', 'dve']

    class WStream:
        def __init__(self):
            self.sched = []
            self.dma_i = 0
            self.cast_i = 0
            self.get_i = 0
            self.nstage = 0

        def extend(self, items):
            self.sched.extend(items)

        def _dma(self):
            i = self.dma_i
            if i >= len(self.sched):
                return
            name, src, a, b, mode, scr = self.sched[i]
            if mode == 'direct':
                l_, bi = scr
                S.dma('sp', wslot[i % NSLOT][:, 0:a * b], wscr[l_][bi][:, 0:a * b],
                      ins=['wscr%d_%d' % (l_, bi)], outs=[wslot[i % NSLOT]])
            else:
                st = wstg[self.nstage % NSTG]
                self.sched[i] = (name, src, a, b, mode, scr, self.nstage % NSTG)
                self.nstage += 1
                S.dma('sp', st[:, 0:a * b].rearrange("p (a b) -> p a b", a=a), src, ins=[], outs=[st])
            self.dma_i += 1

        def _cast(self):
            i = self.cast_i
            if i >= len(self.sched):
                return
            while self.dma_i <= i:
                self._dma()
            it = self.sched[i]
            name, src, a, b, mode, scr = it[:6]
            if mode != 'direct':
                CP(wslot[i % NSLOT][:, 0:a * b], wstg[it[6]][:, 0:a * b], cast_rr[i % 2])
                if scr is not None:
                    l_, bi = scr
                    S.dma('pool', wscr[l_][bi][:, 0:a * b], wslot[i % NSLOT][:, 0:a * b],
                          ins=[wslot[i % NSLOT]], outs=['wscr%d_%d' % (l_, bi)])
            self.cast_i += 1

        def get(self, name):
            i = self.get_i
            assert self.sched[i][0] == name, (self.sched[i][0], name)
            while self.cast_i <= i:
                self._cast()
            while self.dma_i <= i + 2:
                if self.dma_i >= len(self.sched):
                    break
                self._dma()
            if self.cast_i <= i + 1:
                self._cast()
            name, src, a, b = self.sched[i][:4]
            self.get_i += 1
            return wslot[i % NSLOT][:, 0:a * b].rearrange("p (a b) -> p a b", a=a)

    WS = WStream()

    def win_blk(l, col0, ncols=256):
        return (I['w_in'][l][:, col0:col0 + ncols].rearrange("(kc p) n -> p kc n", p=128), 8, ncols)

    def sched_layer_st(l, st):
        items = []
        pre = []
        if st == 0:
            for i in range(12):
                pre.append(("wc%d" % i,
                            I['w_cond'][l][:, i * 256:(i + 1) * 256].rearrange("(kc p) n -> p kc n", p=128), 8, 256,
                            'stage', None))

        def add_in(tag, off, n):
            for i in range(n // 256):
                src, a, b = win_blk(l, off + i * 256)
                items.append(("%s%d" % (tag, i), src, a, b))

        def add_merge(n):
            for half in range(2):
                src = I['w_branch'][l, n][:, half * 512:(half + 1) * 512].rearrange("(kc p) n -> p kc n", p=128)
                items.append(("wb%d_%d" % (n, half), src, 4, 512))
                for q in range(2):
                    src, a, b = win_blk(l, OFF['MERGE'] + n * 1024 + (half * 2 + q) * 256)
                    items.append(("mg%d_%d" % (n, half * 2 + q), src, a, b))

        add_in('lx', OFF['LRU_X'], 512)
        add_in('lg', OFF['LRU_G'], 512)
        add_merge(0)
        add_in('rw', OFF['RWKV'], 1536)
        src, a, b = win_blk(l, OFF['RWKV'] + 1536, 128)
        items.append(('rwlo', src, a, b))
        add_in('rg', OFF['RWKV_G'], 512)
        add_merge(1)
        add_in('gu', OFF['GM_U'], 512)
        add_in('gv', OFF['GM_V'], 512)
        add_in('gg', OFF['GM_G'], 512)
        add_merge(2)
        add_in('ca', OFF['CF_A'], 512)
        add_in('cb', OFF['CF_B'], 512)
        add_in('cg', OFF['CF_G'], 512)
        add_merge(3)
        for i in range(4):
            src = I['w_out'][l][:, i * 256:(i + 1) * 256].rearrange("(kc p) n -> p kc n", p=128)
            items.append(("wo%d" % i, src, 8, 256))
        out = []
        for bi, (nm, src, a, b) in enumerate(items):
            assert bi < NBLK_MAX
            out.append((nm, src, a, b, 'stage' if st == 0 else 'direct', (l, bi)))
        return pre + out

    for st in range(NST):
        for l in range(2):
            WS.extend(sched_layer_st(l, st))

    def cond_setup():
        c = T[0]
        DMA(c[0:NSC, 0:512], I['c33'][:, 0:512])
        DMA(T[1][0:NSC, 0:512], I['c33'][:, 512:1024])
        ACT(c[0:NSC, 0:512], c[0:NSC, 0:512], AF.Silu)
        ACT(T[1][0:NSC, 0:512], T[1][0:NSC, 0:512], AF.Silu)
        for kc in range(8):
            src = (c if kc < 4 else T[1])[0:NSC, (kc % 4) * 128:(kc % 4 + 1) * 128]
            ps = PS()
            TR(ps[:, 0:NSC], src, NSC)
            CP(SCT[:, kc, :], ps[:, 0:NSC], 'act')
            CP(SCPR[:, kc, :], SCT[:, kc, 32:33].broadcast_to([128, 128]))

    def cond_layer(l):
        DMA(T[2][0:NSC, 0:512], I['b_cond'][l:l + 1, 2048:2560].broadcast_to([NSC, 512]))
        DMA(T[3][0:NSC, 0:512], I['b_cond'][l:l + 1, 2560:3072].broadcast_to([NSC, 512]))
        import os
        ksub = int(os.environ.get('KSUB', '12'))
        for i in range(12):
            if i >= ksub:
                return
            wb = WS.get("wc%d" % i)
            if i < 8:
                for ci in range(2):
                    ch = i * 2 + ci
                    ps = PS()
                    for kc in range(8):
                        MM(ps[:, 0:NSC], wb[:, kc, ci * 128:(ci + 1) * 128], SCT[:, kc, :], kc == 0, kc == 7)
                    if ch < 8:
                        TS(modT[l][:, ch, :], ps[:, 0:NSC], pp('bc', ch), None, ALU.add)
                    else:
                        TS(modT[l][:, ch, :], ps[:, 0:NSC], pp('bc', ch), 1.0, ALU.add, ALU.add)
            else:
                g0 = (i - 8) * 256
                ps = PS()
                for kc in range(8):
                    MM(ps[0:NSC, 0:256], SCT[:, kc, :], wb[:, kc, :], kc == 0, kc == 7)
                bt = (T[2] if g0 < 512 else T[3])[0:NSC, (g0 % 512):(g0 % 512) + 256]
                TT(G33[l][:, g0:g0 + 256], ps[0:NSC, 0:256], bt, ALU.add)

    def proj_fm(wb, ci, n, c0):
        ps = PS()
        for kc in range(8):
            MM(ps[:, 0:n], wb[:, kc, ci * 128:(ci + 1) * 128], hT[:, kc, c0:c0 + n], kc == 0, kc == 7)
        return ps

    def fm_to_rows(srcs, n, dst):
        ps = PS()
        for j, s in enumerate(srcs):
            TR(ps[0:n, j * 128:(j + 1) * 128], s, 128)
        CP(dst, ps[0:n, 0:512], 'act')

    evac_rr = {'i': 0}

    def EV(out, in_):
        e = 'act' if evac_rr['i'] % 2 == 0 else 'dve'
        evac_rr['i'] += 1
        CP(out, in_, e)

    def gate_stage(tag, samp, hook=None):
        for b in range(2):
            wb = WS.get("%s%d" % (tag, b))
            for ci in range(2):
                j = b * 2 + ci
                ps = proj_fm(wb, ci, ST, 0)
                ACT(SG[:, j, 0:ST], ps[:, 0:ST], AF.Silu)
                if samp:
                    ps = proj_fm(wb, ci, NS, ST)
                    ACT(SG[:, j, ST:ST + NS], ps[:, 0:NS], AF.Silu)
                if hook is not None:
                    hook()

    def merge_stage(n, samp):
        for half in range(2):
            wbr = WS.get("wb%d_%d" % (n, half))
            for q in range(2):
                wm = WS.get("mg%d_%d" % (n, half * 2 + q))
                for ci in range(2):
                    dc = half * 4 + q * 2 + ci
                    for (c0, nn) in ([(0, ST), (ST, NS)] if samp else [(0, ST)]):
                        psA = proj_fm(wm, ci, nn, c0)
                        psB = PS()
                        for kc in range(4):
                            MM(psB[:, 0:nn], wbr[:, kc, (dc % 4) * 128:(dc % 4 + 1) * 128], oT[:, kc, c0:c0 + nn],
                               kc == 0, kc == 3)
                        sg = T[9][:, 0:nn]
                        ACT(sg, psA[:, 0:nn], AF.Sigmoid)
                        if n == 0:
                            TT(mT[:, dc, c0:c0 + nn], sg, psB[:, 0:nn], ALU.mult)
                        else:
                            TT(T[10][:, 0:nn], sg, psB[:, 0:nn], ALU.mult)
                            TT(mT[:, dc, c0:c0 + nn], T[10][:, 0:nn], mT[:, dc, c0:c0 + nn], ALU.add, eng='pool')

    def build_hT(l, st, samp):
        if l == 0:
            for blk in range(4):
                DMA(Xb[blk][:, :], I['xp'][st * ST + blk * 128: st * ST + (blk + 1) * 128, :])
            if samp:
                DMA(XS[:, :], I['xs'])
        for blk in range(4):
            for kg in range(2):
                ps = PS()
                for q in range(4):
                    kc = kg * 4 + q
                    TR(ps[:, q * 128:(q + 1) * 128], Xb[blk][:, kc * 128:(kc + 1) * 128], 128)
                for q in range(4):
                    kc = kg * 4 + q
                    ACT(hT[:, kc, blk * 128:(blk + 1) * 128], ps[:, q * 128:(q + 1) * 128], AF.Identity,
                        bias=modT[l][:, kc, 32:33], scale=modT[l][:, 8 + kc, 32:33])
        if samp:
            for kc in range(8):
                ps2 = PS()
                TR(ps2[:, 0:NS], XS[:, kc * 128:(kc + 1) * 128], NS)
                TT(T[0][:, 0:NS], ps2[:, 0:NS], modT[l][:, 8 + kc, 0:NS], ALU.mult)
                TT(hT[:, kc, ST:ST + NS], T[0][:, 0:NS], modT[l][:, kc, 0:NS], ALU.add)

    def lru_branch(l, st, samp):
        for b in range(2):
            wb = WS.get("lx%d" % b)
            for ci in range(2):
                j = b * 2 + ci
                ps = proj_fm(wb, ci, ST, 0)
                EV(PJ[j][:, 3:3 + ST], ps[:, 0:ST])
                if samp:
                    ps = proj_fm(wb, ci, NS, ST)
                    EV(PJs[:, j, :], ps[:, 0:NS])
        gate_stage('lg', samp)
        if samp:
            LB = T[8]
            LBv = LB[:, 0:192].rearrange("p (j s k) -> p j s k", j=4, s=NS)
            DMA(T[7][0:48, 0:512], I['st_lc'][l].rearrange("s k c -> (s k) c"))
            DMA(T[6][0:NS, 0:512], I['st_lh'][l])
            ps = PS()
            ps2 = PS()
            for j in range(4):
                TR(ps[:, j * 48:(j + 1) * 48], T[7][0:48, j * 128:(j + 1) * 128], 48)
                TR(ps2[:, j * NS:(j + 1) * NS], T[6][0:NS, j * 128:(j + 1) * 128], NS)
            CP(LB[:, 0:192], ps[:, 0:192], 'act')
            CP(LB[:, 192:256], ps2[:, 0:64], 'act')
            H0v = LB[:, 192:256].rearrange("p (j s) -> p j s", j=4)
            S.dma('sp', O['o_lcs'][l][:, 0:2, :], I['st_lc'][l][:, 1:3, :], ins=[], outs=['o_lcs_a'])
        def parts(is_s, n, j, ts):
            xc_t, r_t, ig_t, a_t, a2_t, xcb_t = ts
            xc = xc_t[:, 0:n]
            xcb = xcb_t[:, 0:n]
            r = r_t[:, 0:n]
            ig = ig_t[:, 0:n]
            a = a_t[:, 0:n]
            a2 = a2_t[:, 0:n]
            h = r_t[:, 0:n]
            box = {}

            def A1():
                if not is_s:
                    xb = PJ[j]
                    CP(xb[:, 0:3], LCH[l][:, j, :], 'pool')
                    TS(xc, xb[:, 3:3 + n], pp('lcw', 12 + j), pp('lcb', j), ALU.mult, ALU.add)
                    for k in range(3):
                        STT(xc, xb[:, k:k + n], pp('lcw', k * 4 + j), xc, ALU.mult, ALU.add)
                    CP(LCH[l][:, j, :], xb[:, ST:ST + 3], 'pool')
                else:
                    TS(xc, PJs[:, j, :], pp('lcw', 12 + j), pp('lcb', j), ALU.mult, ALU.add)
                    for k in range(3):
                        STT(xc, LBv[:, j, :, k], pp('lcw', k * 4 + j), xc, ALU.mult, ALU.add)

            def A2():
                CP(xcb, xc, 'act')
                box['psr'] = PS()
                MM(box['psr'][:, 0:n], LP.WRI[:, j, :], xcb)
                box['psi'] = PS()
                MM(box['psi'][:, 0:n], LP.WRI[:, 4 + j, :], xcb)

            def B1():
                ACT(r, box['psr'][:, 0:n], AF.Sigmoid, bias=pp('lbr', j))
                ACT(ig, box['psi'][:, 0:n], AF.Sigmoid, bias=pp('lbi', j))
                ACT(a, r, AF.Exp, scale=LP.DER[:, j:j + 1])
                ACT(a2, r, AF.Exp, scale=LP.DER[:, 4 + j:5 + j])
                ACT(a2, a2, AF.Sqrt, bias=1.0, scale=-1.0)

            def B2():
                TT(ig, ig, a2, ALU.mult)
                TT(ig, ig, xc, ALU.mult)
                if not is_s:
                    xb = PJ[j]
                    S.op('dve', lambda e: e.tensor_tensor_scan(out=h, data0=a, data1=ig, initial=LH[l][:, j:j + 1],
                                                               op0=ALU.mult, op1=ALU.add),
                         ins=[a, ig, LH[l]], outs=[h])
                    CP(LH[l][:, j:j + 1], h[:, n - 1:n], 'pool')
                    if st == NST - 1:
                        DMA(O['o_lhp'][l, j * 128:(j + 1) * 128].rearrange("(p o) -> p o", o=1), h[:, n - 1:n])
                        CP(T[5][:, j * 3:(j + 1) * 3], xb[:, ST:ST + 3], 'pool')
                else:
                    TT(h, a, H0v[:, j, :], ALU.mult)
                    TT(h, h, ig, ALU.add)
                    CP(T[5][:, j * NS:(j + 1) * NS], h, 'pool')
                c0 = ST if is_s else 0
                TT(oT[:, j, c0:c0 + n], h, SG[:, j, c0:c0 + n], ALU.mult)

            return A1, A2, B1, B2

        tsets = [(T[0], T[1], T[2], T[3], T[4], TB[0]), (PJ[4], PJ[5], PJ[6], PJ[7], PJ[8], TB[1])]
        for (is_s, n) in ([(False, ST), (True, NS)] if samp else [(False, ST)]):
            if is_s:
                for j in range(4):
                    for f in parts(True, n, j, tsets[0]):
                        f()
            else:
                P = [parts(False, n, j, tsets[j % 2]) for j in range(4)]
                P[0][0]()
                P[0][1]()
                for j in range(4):
                    if j < 3:
                        P[j + 1][0]()
                    P[j][2]()
                    if j < 3:
                        P[j + 1][1]()
                    P[j][3]()
            if is_s:
                fm_to_rows([T[5][:, j * NS:(j + 1) * NS] for j in range(4)], NS, T[6][0:NS, 0:512])
                DMA(O['o_lhs'][l], T[6][0:NS, 0:512])
                fm_to_rows([PJs[:, j, :] for j in range(4)], NS, T[7][0:NS, 0:512])
                DMA(O['o_lcs'][l][:, 2, :], T[7][0:NS, 0:512])
            elif st == NST - 1:
                fm_to_rows([T[5][:, j * 3:(j + 1) * 3] for j in range(4)], 3, T[6][0:3, 0:512])
                DMA(O['o_lcp'][l], T[6][0:3, 0:512])
        merge_stage(0, samp)

    def conf_branch(l, st, samp):
        for b in range(2):
            wb = WS.get("ca%d" % b)
            for ci in range(2):
                j = b * 2 + ci
                ps = proj_fm(wb, ci, ST, 0)
                EV(PJ[j][:, 30:30 + ST], ps[:, 0:ST])
                if samp:
                    ps = proj_fm(wb, ci, NS, ST)
                    EV(PJs[:, j, :], ps[:, 0:NS])
        for b in range(2):
            wb = WS.get("cb%d" % b)
            for ci in range(2):
                j = b * 2 + ci
                ps = proj_fm(wb, ci, ST, 0)
                ACT(T[0][:, 0:ST], ps[:, 0:ST], AF.Sigmoid)
                TT(PJ[j][:, 30:30 + ST], PJ[j][:, 30:30 + ST], T[0][:, 0:ST], ALU.mult)
                if samp:
                    ps = proj_fm(wb, ci, NS, ST)
                    ACT(T[0][:, 0:NS], ps[:, 0:NS], AF.Sigmoid)
                    TT(PJs[:, j, :], PJs[:, j, :], T[0][:, 0:NS], ALU.mult)
        gate_stage('cg', samp)
        Y = [PJ[4 + j] for j in range(4)]
        Ys = PJs[:, 4:8, :]
        DW = AM[:, :, :].rearrange("p a b -> p (a b)")[:, 0:31 * 128].rearrange("p (k c) -> p k c", k=31)
        if samp:
            for g in range(4):
                DMA(T[g][0:120, 0:512], I['st_cf'][l][g * 4:(g + 1) * 4].rearrange("s k c -> (s k) c"))
            for j in range(4):
                ps = PS()
                for g in range(4):
                    TR(ps[:, g * 120:(g + 1) * 120], T[g][0:120, j * 128:(j + 1) * 128], 120)
                CP(PJ[8 + j][:, 0:480], ps[:, 0:480], 'act')
            S.dma('sp', O['o_cfs'][l][:, 0:29, :], I['st_cf'][l][:, 1:30, :], ins=[], outs=['o_cfs_a'])
        for j in range(4):
            gl = PJ[j]
            CP(gl[:, 0:30], CGC[l][:, j, :], 'pool')
            glb = TB[j % 2]
            CP(glb[:, 0:30 + ST], gl[:, 0:30 + ST], 'act')
            if j % 2 == 0:
                TT(DW, identb[:, :].unsqueeze(1).broadcast_to([128, 31, 128]),
                   LP.CW[:, j * 31:(j + 1) * 31].unsqueeze(2).broadcast_to([128, 31, 128]), ALU.mult)
                taps = [DW[:, k, :] for k in range(31)]
            else:
                dwa = PQ[0][:, :, :].rearrange("p a b -> p (a b)").rearrange("p (k c) -> p k c", k=16)
                dwb = PQ[1][:, :, :].rearrange("p a b -> p (a b)")[:, 0:15 * 128].rearrange("p (k c) -> p k c", k=15)
                TT(dwa, identb[:, :].unsqueeze(1).broadcast_to([128, 16, 128]),
                   LP.CW[:, j * 31:j * 31 + 16].unsqueeze(2).broadcast_to([128, 16, 128]), ALU.mult)
                TT(dwb, identb[:, :].unsqueeze(1).broadcast_to([128, 15, 128]),
                   LP.CW[:, j * 31 + 16:(j + 1) * 31].unsqueeze(2).broadcast_to([128, 15, 128]), ALU.mult)
                taps = [dwa[:, k, :] for k in range(16)] + [dwb[:, k, :] for k in range(15)]
            ps = PS()
            for k in range(31):
                MM(ps[:, 0:ST], taps[k], glb[:, k:k + ST], k == 0, k == 30)
            ACT(Y[j][:, 0:ST], ps[:, 0:ST], AF.Identity, bias=pp('cfb', j))
            CP(CGC[l][:, j, :], gl[:, ST:ST + 30], 'pool')
            if st == NST - 1:
                CP(T[5][:, j * 30:(j + 1) * 30], gl[:, ST:ST + 30], 'pool')
            if samp:
                cb = PJ[8 + j][:, 0:480].rearrange("p (s k) -> p s k", s=NS)
                wbc = LP.CW[:, j * 31:j * 31 + 30].unsqueeze(1).broadcast_to([128, NS, 30])
                tmp = T[0][:, 0:480].rearrange("p (s k) -> p s k", s=NS)
                TT(tmp, cb, wbc, ALU.mult)
                RED(T[1][:, 0:NS], tmp)
                STT(T[1][:, 0:NS], PJs[:, j, :], LP.CW[:, j * 31 + 30:j * 31 + 31], T[1][:, 0:NS], ALU.mult, ALU.add)
                TS(Ys[:, j, :], T[1][:, 0:NS], pp('cfb', j), None, ALU.add)
        if st == NST - 1:
            fm_to_rows([T[5][:, j * 30:(j + 1) * 30] for j in range(4)], 30, T[6][0:30, 0:512])
            DMA(O['o_cfp'][l], T[6][0:30, 0:512])
        if samp:
            fm_to_rows([PJs[:, j, :] for j in range(4)], NS, T[7][0:NS, 0:512])
            DMA(O['o_cfs'][l][:, 29, :], T[7][0:NS, 0:512])
        for (is_s, n) in ([(False, ST), (True, NS)] if samp else [(False, ST)]):
            ys = [(Ys[:, j, :] if is_s else Y[j][:, 0:n]) for j in range(4)]
            ps1 = PS()
            ps2 = PS()
            for j in range(4):
                MM(ps1[:, 0:n], ones[:], ys[j], j == 0, j == 3)
            for j in range(4):
                ACT(T[j][:, 0:n], ys[j], AF.Square)
            for j in range(4):
                MM(ps2[:, 0:n], ones[:], T[j][:, 0:n], j == 0, j == 3)
            mean = T[4][:, 0:n]
            rstd = T[5][:, 0:n]
            ACT(mean, ps1[:, 0:n], AF.Copy, scale=1.0 / 512)
            TT(T[6][:, 0:n], mean, mean, ALU.mult)
            STT(rstd, ps2[:, 0:n], 1.0 / 512, T[6][:, 0:n], ALU.mult, ALU.subtract)
            TS(rstd, rstd, 0.0, 1e-5, ALU.max, ALU.add)
            ACT(rstd, rstd, AF.Sqrt)
            RECIP(rstd, rstd)
            c0 = ST if is_s else 0
            for j in range(4):
                t = T[j][:, 0:n]
                TT(t, ys[j], mean, ALU.subtract)
                TT(t, t, rstd, ALU.mult)
                ACT(t, t, AF.Silu, bias=pp('cfbb', j), scale=pp('cfg', j))
                TT(oT[:, j, c0:c0 + n], t, SG[:, j, c0:c0 + n], ALU.mult)
        merge_stage(3, samp)

    def gmlp_branch(l, st, samp):
        for b in range(2):
            wb = WS.get("gu%d" % b)
            for ci in range(2):
                j = b * 2 + ci
                ps = proj_fm(wb, ci, ST, 0)
                EV(PJ[j][:, 0:ST], ps[:, 0:ST])
                if samp:
                    ps = proj_fm(wb, ci, NS, ST)
                    EV(PJs[:, j, :], ps[:, 0:NS])
        for b in range(2):
            wb = WS.get("gv%d" % b)
            for blk in range(4):
                ps = PS()
                for kc in range(8):
                    MM(ps[:, 0:256], hT[:, kc, blk * 128:(blk + 1) * 128], wb[:, kc, :], kc == 0, kc == 7)
                EV(PJ[4 + blk][:, b * 256:(b + 1) * 256], ps[:, 0:256])
            if samp:
                ps = PS()
                for kc in range(8):
                    MM(ps[0:NS, 0:256], hT[:, kc, ST:ST + NS], wb[:, kc, :], kc == 0, kc == 7)
                EV(PJ[8][0:NS, b * 256:(b + 1) * 256], ps[0:NS, 0:256])
        gate_stage('gg', samp)
        DMA(T[5][:, 0:512], I['gmlp_ln_g'][l:l + 1, :].broadcast_to([128, 512]))
        DMA(T[6][:, 0:512], I['gmlp_ln_b'][l:l + 1, :].broadcast_to([128, 512]))
        GLN = [T[5], T[6]]
        VN = TB[0:4]
        for blk in range(5 if samp else 4):
            np_ = 128 if blk < 4 else NS
            v = (PJ[4 + blk] if blk < 4 else PJ[8])[0:np_, 0:512]
            st6 = SM[0:np_, 0:6]
            mv = SM[0:np_, 8:10]
            S.op('dve', lambda e: e.bn_stats(st6, v), ins=[v], outs=[st6])
            S.op('dve', lambda e: e.bn_aggr(mv, st6), ins=[st6], outs=[mv])
            rs = SM[0:np_, 10:11]
            TS(rs, SM[0:np_, 9:10], 1e-5, None, ALU.add)
            ACT(rs, rs, AF.Sqrt)
            RECIP(rs, rs)
            TS(v, v, SM[0:np_, 8:9], rs, ALU.subtract, ALU.mult)
            TT(v, v, GLN[0][0:np_, 0:512], ALU.mult)
            if blk < 4:
                TT(VN[blk][:, 0:512], v, GLN[1][:, 0:512], ALU.add)
            else:
                TT(v, v, GLN[1][0:np_, 0:512], ALU.add)
                DMA(O['o_gv'][l], v)
        for j in range(4):
            z = T[j]
            for blk in range(4):
                ps = PS()
                MM(ps[:, 0:128], VN[blk][:, j * 128:(j + 1) * 128], LP.wmT[:, 2 * j, :])
                MM(ps[:, 128:256], VN[blk][:, j * 128:(j + 1) * 128], LP.wmT[:, 2 * j + 1, :])
                TT(z[0:64, blk * 128:(blk + 1) * 128], ps[0:64, 0:128], LP.GBt[0:64, j, :], ALU.add)
                TT(z[64:128, blk * 128:(blk + 1) * 128], ps[64:128, 128:256], LP.GBt[64:128, j, :], ALU.add)
            TT(z[:, 0:ST], z[:, 0:ST], PJ[j][:, 0:ST], ALU.mult)
            TT(oT[:, j, 0:ST], z[:, 0:ST], SG[:, j, 0:ST], ALU.mult)
        if samp:
            ps = PS()
            for j in range(4):
                TR(ps[:, j * NS:(j + 1) * NS], PJ[8][0:NS, j * 128:(j + 1) * 128], NS)
            for j in range(4):
                z = T[4][:, 0:NS]
                TS(z, ps[:, j * NS:(j + 1) * NS], LP.GS0[:, j:j + 1], LP.GS0[:, 4 + j:5 + j], ALU.mult, ALU.add)
                TT(z, z, PJs[:, j, :], ALU.mult)
                TT(oT[:, j, ST:ST + NS], z, SG[:, j, ST:ST + NS], ALU.mult)
        merge_stage(2, samp)

    AR = S.sb("AR", [128, 4, 2, 128], BF16)
    TOK = S.sb("TOK", [128, 4, 384], BF16)
    AM = S.sb("AM", [128, 8, 512], BF16)
    PQ = [S.sb("PQ%d" % i, [128, 8, 256], BF16) for i in range(2)]
    YY = [S.sb("YY%d" % i, [128, 8, 128], BF16) for i in range(2)]
    OTK = S.sb("OTK", [128, 4, 128], F32)
    RH = S.sb("RH", [128, 128], BF16)
    UU = S.sb("UU", [128, 128], BF16)
    SM2 = S.sb("SM2", [128, 32], F32)
    SO = S.sb("SO", [128, 128], F32)

    def head_norm(x3, np_, ng, sq_tile=None):
        sm = SM2[0:np_, 0:ng]
        RED(sm, x3)
        TS(sm, sm, 1.0 / 64, None, ALU.mult)
        TT(x3, x3, sm.unsqueeze(2).broadcast_to([np_, ng, 64]), ALU.subtract)
        sq = (sq_tile if sq_tile is not None else T[10])[0:np_, 0:ng * 64].rearrange("p (g k) -> p g k", g=ng)
        TT(sq, x3, x3, ALU.mult)
        vs = SM2[0:np_, 8:8 + ng]
        RED(vs, sq)
        TS(vs, vs, 1.0 / 64, 64e-5, ALU.mult, ALU.add)
        ACT(vs, vs, AF.Sqrt)
        RECIP(vs, vs)
        TT(x3, x3, vs.unsqueeze(2).broadcast_to([np_, ng, 64]), ALU.mult)

    import os as _os
    _kr = int(_os.environ.get('KR', '99'))

    def ck(n):
        return _kr <= n

    def rwkv_branch(l, st, samp):
        def shift_ops(ch):
            p = PJ[ch]
            d = T[ch % 2]
            CP(p[:, 0:1], SHC[l][:, ch:ch + 1], 'pool')
            CP(SHC[l][:, ch:ch + 1], p[:, ST:ST + 1], 'pool')
            if st == NST - 1:
                DMA(O['o_shp'][l, ch * 128:(ch + 1) * 128].rearrange("(p o) -> p o", o=1), p[:, ST:ST + 1])
            TT(d[:, 0:ST], p[:, 0:ST], p[:, 1:1 + ST], ALU.subtract)
            STT(p[:, 1:1 + ST], d[:, 0:ST], pp('mu', ch), p[:, 1:1 + ST], ALU.mult, ALU.add)

        for b in range(6):
            wb = WS.get("rw%d" % b)
            for ci in range(2):
                ch = b * 2 + ci
                ps = proj_fm(wb, ci, ST, 0)
                EV(PJ[ch][:, 1:1 + ST], ps[:, 0:ST])
                if samp:
                    ps = proj_fm(wb, ci, NS, ST)
                    EV(PJs[:, ch, :], ps[:, 0:NS])
                shift_ops(ch)
        wb = WS.get("rwlo")
        ps = proj_fm(wb, 0, ST, 0)
        EV(PJ[12][:, 1:1 + ST], ps[:, 0:ST])
        if samp:
            ps = proj_fm(wb, 0, NS, ST)
            EV(PJs[:, 12, :], ps[:, 0:NS])
        shift_ops(12)
        lo = TB[0]

        def lo_compute():
            ACT(lo[0:64, 0:ST], PJ[12][0:64, 1:1 + ST], AF.Tanh)
            CP(lo[64:128, 0:ST], PJ[12][64:128, 1:1 + ST], 'act')

        n = ST
        yfi = {}

        def E1(j):
            r = PJ[j][:, 1:1 + n]
            k0 = PJ[4 + j][:, 1:1 + n]
            v = PJ[8 + j][:, 1:1 + n]
            sb = 16 + 12 * (j % 2)
            sg = T[1][:, 0:n]
            a = T[2][:, 0:n]
            ps = PS()
            MM(ps[:, 0:n], LP.WW[0:64, j * 128:(j + 1) * 128], lo[0:64, 0:n])
            ACT(sg, ps[:, 0:n], AF.Sigmoid, bias=pp('w0', j))
            ps = PS()
            MM(ps[:, 0:n], LP.WW[64:128, j * 128:(j + 1) * 128], lo[64:128, 0:n])
            ACT(a, ps[:, 0:n], AF.Sigmoid, bias=pp('a0', j))
            yield
            kk = T[3][:, 0:n]
            TS(kk, k0, pp('kk', j), None, ALU.mult)
            TT(T[4][:, 0:n], kk, kk, ALU.mult)
            ps = PS()
            MM(ps[:, 0:n], bones[:], T[4][:, 0:n])
            ACT(T[4][:, 0:n], ps[:, 0:n], AF.Sqrt)
            yield
            TS(T[4][:, 0:n], T[4][:, 0:n], 1e-12, None, ALU.max)
            RECIP(T[4][:, 0:n], T[4][:, 0:n])
            yield
            TT(kk, kk, T[4][:, 0:n], ALU.mult)
            TS(T[4][:, 0:n], a, pp('ka', j), LP.DER[:, 8 + j:9 + j], ALU.mult, ALU.add)
            yield
            k = T[5][:, 0:n]
            TT(k, k0, T[4][:, 0:n], ALU.mult)
            beta = T[6][:, 0:n]
            TT(beta, kk, a, ALU.mult)
            yield
            TT(T[4][:, 0:n], r, k, ALU.mult)
            TS(BRt[:, :], bones[:], pp('rk', j), None, ALU.mult)
            ps = PS()
            MM(ps[:, 0:n], BRt[:, :], T[4][:, 0:n])
            TT(k0, ps[:, 0:n], v, ALU.mult)
            yield
            cs = T[4]
            for c in range(4):
                S.op('dve', lambda e: e.tensor_tensor_scan(out=cs[:, c * 128:(c + 1) * 128], data0=ones[:, 0:128],
                                                           data1=sg[:, c * 128:(c + 1) * 128], initial=0.0,
                                                           op0=ALU.mult, op1=ALU.add),
                     ins=[ones, sg], outs=[cs])
                if c % 2 == 1:
                    yield
            cs3 = cs[:, 0:n].rearrange("p (c t) -> p c t", c=4)
            dcs = T[8][:, 0:n]
            dcs3 = dcs.rearrange("p (c t) -> p c t", c=4)
            TT(dcs3, cs3, cs3[:, :, 63:64].broadcast_to([128, 4, 128]), ALU.subtract)
            f1 = SM[:, sb:sb + 4]
            gL = SM[:, sb + 4:sb + 8]
            f2 = SM[:, sb + 8:sb + 12]
            ACT(f1, cs3[:, :, 63], AF.Exp, scale=-CDEC)
            ACT(gL, cs3[:, :, 127], AF.Exp, scale=-CDEC)
            TT(f2, cs3[:, :, 127], cs3[:, :, 63], ALU.subtract)
            ACT(f2, f2, AF.Exp, scale=-CDEC)
            yield
            E2_ = T[9][:, 0:n]
            E3 = T[10][:, 0:n]
            ACT(E2_, dcs, AF.Exp, scale=-CDEC)
            ACT(E3, dcs, AF.Exp, scale=CDEC)
            TT(dcs, dcs, sg, ALU.subtract)
            ACT(dcs, dcs, AF.Exp, scale=-CDEC)
            yield
            STT(dcs, kk, -1.0, dcs, ALU.mult, ALU.mult)
            TT(E2_, r, E2_, ALU.mult)
            yield
            TT(beta, beta, E3, ALU.mult)
            TT(k, k, E3, ALU.mult)
            yield

        def E2(j):
            v = PJ[8 + j][:, 1:1 + n]
            dcs3 = T[8][:, 0:n].rearrange("p (c t) -> p c t", c=4)
            beta = T[6][:, 0:n]
            k = T[5][:, 0:n]
            CP(AR[:, :, 0, :], dcs3, 'pool')
            CP(AR[:, :, 1, :], T[9][:, 0:n].rearrange("p (c t) -> p c t", c=4), 'pool')
            CP(TB[1][:, 0:n], beta, 'pool')
            CP(TB[2][:, 0:n], k, 'pool')
            for c in range(4):
                ps = PS()
                TR(ps[:, 0:128], v[:, c * 128:(c + 1) * 128], 128)
                TR(ps[:, 128:256], beta[:, c * 128:(c + 1) * 128], 128)
                TR(ps[:, 256:384], k[:, c * 128:(c + 1) * 128], 128)
                CP(TOK[:, c, :], ps[:, 0:384], 'act')
            for c in range(4):
                for h in range(2):
                    p_ = c * 2 + h
                    hp = slice(h * 64, (h + 1) * 64)
                    arf = AR[hp, c, :, :].rearrange("p a t -> p (a t)")
                    psX = PS()
                    MM(psX[:, 0:256], TB[1][hp, c * 128:(c + 1) * 128], arf)
                    MM(psX[:, 256:512], TB[2][hp, c * 128:(c + 1) * 128], arf)
                    TT(AM[:, p_, :], psX[:, :], mask4[:, :], ALU.mult)
                    psY = PS()
                    MM(psY[:, 0:128], AR[hp, c, 0, :], TB[1][hp, c * 128:(c + 1) * 128])
                    TT(PQ[0][:, p_, 0:128], psY[:, 0:128], mls[:, :], ALU.mult)
            CP(PQ[0][:, :, 128:256], AM[:, :, 0:128], 'pool')
            TT(YY[0][:, :, :], AM[:, :, 0:128], identb[:, :].unsqueeze(1).broadcast_to([128, 8, 128]), ALU.add)
            cur = 0
            ycur = 0

            def y_update(pq, yc):
                for p4 in range(2):
                    ps = PS()
                    for q in range(4):
                        p_ = p4 * 4 + q
                        MM(ps[:, q * 128:(q + 1) * 128], pq[:, p_, 0:128], YY[yc][:, p_, :])
                    TT(YY[1 - yc][:, p4 * 4:p4 * 4 + 4, :], ps[:, :].rearrange("p (a t) -> p a t", a=4),
                       YY[yc][:, p4 * 4:p4 * 4 + 4, :], ALU.add)

            PQ3 = [PQ[0], PQ[1]]
            for lvl in range(1, 7):
                nw = 1 - cur
                for p2 in range(4):
                    ps = PS()
                    for q in range(2):
                        p_ = p2 * 2 + q
                        MM(ps[:, q * 256:q * 256 + 128], PQ3[cur][:, p_, 128:256], PQ3[cur][:, p_, 0:128])
                        if lvl < 6:
                            MM(ps[:, q * 256 + 128:q * 256 + 256], PQ3[cur][:, p_, 0:128], PQ3[cur][:, p_, 128:256])
                    if lvl < 6:
                        CP(PQ3[nw][:, p2 * 2:p2 * 2 + 2, :], ps[:, :].rearrange("p (a t) -> p a t", a=2), 'act')
                    else:
                        CP(PQ3[nw][:, p2 * 2:p2 * 2 + 2, 0:128],
                           ps[:, :].rearrange("p (a t) -> p a t", a=2)[:, :, 0:128], 'act')
                if lvl >= 2:
                    y_update(PQ3[cur], ycur)
                    ycur = 1 - ycur
                cur = nw
            y_update(PQ3[cur], ycur)
            ycur = 1 - ycur
            cur = ycur
            yfi[j] = cur

        def Q(j):
            sb = 16 + 12 * (j % 2)
            Yf = YY[yfi[j]]
            tmp = PJ[j][:, 0:128]
            bonus = PJ[4 + j][:, 1:1 + n]
            for c in range(4):
                TS(SIGf[:, :], SIG[l][:, j, :], SM[:, sb + c:sb + c + 1], None, ALU.mult)
                ps1 = PS()
                MM(ps1[:, 0:128], AR[:, c, 0, :], SIGf[:, :], True, False)
                for h in range(2):
                    hc = slice(h * 64, (h + 1) * 64)
                    MM(ps1[:, hc], AM[:, c * 2 + h, 256:384], TOK[:, c, hc], False, h == 1)
                CP(RH[:, :], ps1[:, 0:128], 'act')
                yield
                ps2 = PS()
                for h in range(2):
                    hc = slice(h * 64, (h + 1) * 64)
                    MM(ps2[:, hc], Yf[:, c * 2 + h, :], RH[:, hc])
                CP(UU[:, :], ps2[:, 0:128], 'act')
                yield
                ps3 = PS()
                MM(ps3[:, 0:128], AR[:, c, 1, :], SIGf[:, :], True, False)
                for h in range(2):
                    hc = slice(h * 64, (h + 1) * 64)
                    MM(ps3[:, hc], AM[:, c * 2 + h, 128:256], UU[:, hc], False, False)
                    MM(ps3[:, hc], AM[:, c * 2 + h, 384:512], TOK[:, c, hc], False, h == 1)
                CP(OTK[:, c, :], ps3[:, 0:128], 'act')
                ps4 = PS()
                MM(ps4[:, 0:128], TOK[:, c, 128:256], UU[:, :], True, False)
                MM(ps4[:, 0:128], TOK[:, c, 256:384], TOK[:, c, 0:128], False, True)
                STT(tmp, ps4[:, 0:128], SM[:, sb + 8 + c:sb + 9 + c], bones[:], ALU.mult, ALU.mult)
                STT(SIG[l][:, j, :], SIG[l][:, j, :], SM[:, sb + 4 + c:sb + 5 + c], tmp, ALU.mult, ALU.add)
                yield
            head_norm(OTK[:, :, :].rearrange("p c (h v) -> p (c h) v", h=2), 128, 8, sq_tile=PJ[j])
            yield
            ps = PS()
            for c in range(4):
                TR(ps[:, c * 128:(c + 1) * 128], OTK[:, c, :], 128)
            ob = PJ[8 + j][:, 1:1 + n]
            ACT(ob, ps[:, 0:n], AF.Identity, bias=pp('lxb', j), scale=pp('lxg', j))
            yield
            TT(ob, ob, bonus, ALU.add)
            TT(oT[:, j, 0:n], ob, SG[:, j, 0:n], ALU.mult)
            if st == NST - 1:
                ps = PS()
                TR(ps[:, 0:128], SIG[l][:, j, :], 128)
                CP(SO[:, :], ps[:, 0:128], 'act')
                for h in range(2):
                    DMA(O['o_Sp'][l, 2 * j + h], SO[h * 64:(h + 1) * 64, h * 64:(h + 1) * 64])
            yield

        if samp:
            gate_stage('rg', samp)
            rwkv_sample(l)
            lo_compute()
            for _ in E1(0):
                pass
        else:
            lo_compute()
            ge0 = E1(0)

            def hook():
                for _ in range(3):
                    try:
                        next(ge0)
                    except StopIteration:
                        break

            gate_stage('rg', samp, hook=hook)
            for _ in ge0:
                pass
        E2(0)
        for j in range(4):
            gq = Q(j)
            ge = E1(j + 1) if j < 3 else None
            alive_q, alive_e = True, ge is not None
            while alive_q or alive_e:
                if alive_q:
                    try:
                        next(gq)
                    except StopIteration:
                        alive_q = False
                if alive_e:
                    try:
                        next(ge)
                    except StopIteration:
                        alive_e = False
            if j < 3:
                E2(j + 1)
        merge_stage(1, samp)

    _ks = int(_os.environ.get('KS', '99'))

    def rwkv_sample(l):
        n = NS
        DMA(T[0][0:NS, 0:512], I['st_sh'][l][:, 0:512])
        DMA(T[1][0:NS, 0:512], I['st_sh'][l][:, 512:1024])
        DMA(T[2][0:NS, 0:512], I['st_sh'][l][:, 1024:1536])
        DMA(T[3][0:NS, 0:128], I['st_sh'][l][:, 1536:1664])
        PV = T[4]
        PVv = PV[:, 0:13 * NS].rearrange("p (c s) -> p c s", c=13)
        ps = PS()
        for ch in range(13):
            src = T[ch // 4][0:NS, (ch % 4) * 128:(ch % 4 + 1) * 128]
            TR(ps[:, ch * NS:(ch + 1) * NS], src, NS)
        CP(PV[:, 0:13 * NS], ps[:, 0:13 * NS], 'act')
        for g in range(4):
            nch = 4 if g < 3 else 1
            ps = PS()
            for q in range(nch):
                TR(ps[0:NS, q * 128:(q + 1) * 128], PJs[:, g * 4 + q, :], 128)
            CP(T[5][0:NS, 0:nch * 128], ps[0:NS, 0:nch * 128], 'act')
            DMA(O['o_shs'][l][:, g * 512:g * 512 + nch * 128], T[5][0:NS, 0:nch * 128])
        if _ks <= 1:
            return
        XSv = T[6][:, 0:13 * NS].rearrange("p (c s) -> p c s", c=13)
        for ch in range(13):
            TT(T[7][:, 0:n], PVv[:, ch, :], PJs[:, ch, :], ALU.subtract)
            STT(XSv[:, ch, :], T[7][:, 0:n], pp('mu', ch), PJs[:, ch, :], ALU.mult, ALU.add)
        lo = TB[0]
        ACT(lo[0:64, 0:n], XSv[0:64, 12, :], AF.Tanh)
        CP(lo[64:128, 0:n], XSv[64:128, 12, :], 'act')
        STG = T[8][:, 0:6 * 4 * NS].rearrange("p (o j s) -> p o j s", o=6, j=4)
        BON = T[9][:, 0:4 * NS].rearrange("p (j s) -> p j s", j=4)
        for j in range(4):
            r = XSv[:, j, :]
            k0 = XSv[:, 4 + j, :]
            v = XSv[:, 8 + j, :]
            sg = T[1][:, 0:n]
            a = T[2][:, 0:n]
            ps = PS()
            MM(ps[:, 0:n], LP.WW[0:64, j * 128:(j + 1) * 128], lo[0:64, 0:n])
            ACT(sg, ps[:, 0:n], AF.Sigmoid, bias=pp('w0', j))
            ps = PS()
            MM(ps[:, 0:n], LP.WW[64:128, j * 128:(j + 1) * 128], lo[64:128, 0:n])
            ACT(a, ps[:, 0:n], AF.Sigmoid, bias=pp('a0', j))
            kk = T[3][:, 0:n]
            TS(kk, k0, pp('kk', j), None, ALU.mult)
            ACT(T[5][:, 0:n], kk, AF.Square)
            ps = PS()
            MM(ps[:, 0:n], bones[:], T[5][:, 0:n])
            ACT(T[5][:, 0:n], ps[:, 0:n], AF.Sqrt)
            TS(T[5][:, 0:n], T[5][:, 0:n], 1e-12, None, ALU.max)
            RECIP(T[5][:, 0:n], T[5][:, 0:n])
            TT(kk, kk, T[5][:, 0:n], ALU.mult)
            TS(T[5][:, 0:n], a, pp('ka', j), LP.DER[:, 8 + j:9 + j], ALU.mult, ALU.add)
            TT(STG[:, 2, j, :], k0, T[5][:, 0:n], ALU.mult)
            CP(STG[:, 0, j, :], r, 'pool')
            ACT(STG[:, 1, j, :], sg, AF.Exp, scale=-CDEC)
            CP(STG[:, 3, j, :], v, 'pool')
            TS(STG[:, 4, j, :], kk, -1.0, None, ALU.mult)
            TT(STG[:, 5, j, :], kk, a, ALU.mult)
            TT(T[5][:, 0:n], r, STG[:, 2, j, :], ALU.mult)
            TS(BRt[:, :], bones[:], pp('rk', j), None, ALU.mult)
            ps = PS()
            MM(ps[:, 0:n], BRt[:, :], T[5][:, 0:n])
            TT(BON[:, j, :], ps[:, 0:n], v, ALU.mult)
        if _ks <= 2:
            return
        ROWS = T[10]
        for o in range(6):
            ps = PS()
            for j in range(4):
                TR(ps[0:NS, j * 128:(j + 1) * 128], STG[:, o, j, :], 128)
            CP(ROWS[0:NS, 0:512], ps[0:NS, 0:512], 'act')
            S.dma('sp', scr_in[o], ROWS[0:NS, 0:512], ins=[ROWS], outs=['scr_in'])
        OPS = T[0][:, 0:384].rearrange("p (o k) -> p o k", o=6)
        S.dma('sp', OPS, scr_in.rearrange("o s (h k) -> (s h) o k", h=8), ins=['scr_in'], outs=[T[0]])
        if _ks <= 3:
            return
        rr, ww_, kk_, vv, nk, ka_ = [OPS[:, o, :] for o in range(6)]
        osh = T[1][:, 0:64]
        Sv = I['st_S'][l].rearrange("s h v k -> (s h) v k")
        So = O['o_Ss'][l].rearrange("s h v k -> (s h) v k")
        for q in range(4):
            S0 = T[2][:, 0:512].rearrange("p (v k) -> p v k", v=8)
            for half in range(2):
                vq = q * 16 + half * 8
                S0 = T[2 + half][:, 0:512].rearrange("p (v k) -> p v k", v=8)
                tmp = T[4 + half][:, 0:512].rearrange("p (v k) -> p v k", v=8)
                DMA(S0, Sv[:, vq:vq + 8, :])
                bk = lambda x: x.unsqueeze(1).broadcast_to([128, 8, 64])
                TT(tmp, S0, bk(nk), ALU.mult)
                sa = SM2[:, 16 + half * 8:24 + half * 8]
                RED(sa, tmp)
                TT(S0, S0, bk(ww_), ALU.mult)
                TT(tmp, sa.unsqueeze(2).broadcast_to([128, 8, 64]), bk(ka_), ALU.mult)
                TT(S0, S0, tmp, ALU.add)
                TT(tmp, vv[:, vq:vq + 8].unsqueeze(2).broadcast_to([128, 8, 64]), bk(kk_), ALU.mult)
                TT(S0, S0, tmp, ALU.add)
                TT(tmp, S0, bk(rr), ALU.mult)
                RED(osh[:, vq:vq + 8], tmp)
                DMA(So[:, vq:vq + 8, :], S0)
        if _ks <= 4:
            return
        S.dma('sp', scr_o, osh, ins=[T[1]], outs=['scr_o'])
        OR = T[2][0:NS, 0:512]
        S.dma('sp', OR, scr_o.rearrange("(s h) v -> s (h v)", h=8), ins=['scr_o'], outs=[T[2]])
        if _ks <= 5:
            return
        head_norm(OR.rearrange("s (h v) -> s h v", h=8), NS, 8)
        if _ks <= 6:
            return
        ps = PS()
        for j in range(4):
            TR(ps[:, j * NS:(j + 1) * NS], T[2][0:NS, j * 128:(j + 1) * 128], NS)
        for j in range(4):
            ob = T[3][:, 0:n]
            ACT(ob, ps[:, j * NS:(j + 1) * NS], AF.Identity, bias=pp('lxb', j), scale=pp('lxg', j))
            TT(ob, ob, BON[:, j, :], ALU.add)
            TT(oT[:, j, ST:ST + n], ob, SG[:, j, ST:ST + n], ALU.mult)

    def out_stage(l, st, samp):
        GB = [T[0], T[1]]
        LN = [[PJ[2 * k_ + h_] for h_ in range(2)] for k_ in range(3)]
        for k_, nm in enumerate(['b_out', 'ln_g', 'ln_b']):
            for h_ in range(2):
                DMA(LN[k_][h_][:, 0:512], I[nm][l:l + 1, h_ * 512:(h_ + 1) * 512].broadcast_to([128, 512]))
        for hh in range(2):
            ps = PS()
            MM(ps[:, 0:512], ones[32:33, 0:128], G33[l][32:33, hh * 512:(hh + 1) * 512])
            CP(GB[hh][:, 0:512], ps[:, 0:512], 'act')
        def final_ln(blk):
            np_ = 128 if blk < 4 else NS
            xv = Xb[blk][:, :] if blk < 4 else XS[:, :]
            st6 = SM[0:np_, 32:44]
            mv = SM[0:np_, 44:46]
            S.op('dve', lambda e: e.bn_stats(st6[:, 0:6], xv[:, 0:512]), ins=[xv], outs=[st6])
            S.op('dve', lambda e: e.bn_stats(st6[:, 6:12], xv[:, 512:1024]), ins=[xv], outs=[st6])
            S.op('dve', lambda e: e.bn_aggr(mv, st6), ins=[st6], outs=[mv])
            rs = SM[0:np_, 46:47]
            TS(rs, SM[0:np_, 45:46], 1e-5, None, ALU.add)
            ACT(rs, rs, AF.Sqrt)
            RECIP(rs, rs)
            TS(xv, xv, SM[0:np_, 44:45], rs, ALU.subtract, ALU.mult)
            for h_ in range(2):
                TT(xv[:, h_ * 512:(h_ + 1) * 512], xv[:, h_ * 512:(h_ + 1) * 512], LN[1][h_][0:np_, 0:512], ALU.mult)
                TT(xv[:, h_ * 512:(h_ + 1) * 512], xv[:, h_ * 512:(h_ + 1) * 512], LN[2][h_][0:np_, 0:512], ALU.add)
            if l == DEPTH - 1:
                if blk < 4:
                    DMA(O['o_yp'][st * ST + blk * 128: st * ST + (blk + 1) * 128, :], xv)
                else:
                    DMA(O['o_ys'], xv)

        for i in range(4):
            wo = WS.get("wo%d" % i)
            c0 = i * 256
            for blk in range(5 if samp else 4):
                np_ = 128 if blk < 4 else NS
                ps = PS()
                for kc in range(8):
                    lhs = mT[:, kc, blk * 128:(blk + 1) * 128] if blk < 4 else mT[:, kc, ST:ST + NS]
                    MM(ps[0:np_, 0:256], lhs, wo[:, kc, :], kc == 0, kc == 7)
                t = T[2][0:np_, 0:256]
                TT(t, ps[0:np_, 0:256], LN[0][c0 // 512][0:np_, (c0 % 512):(c0 % 512) + 256], ALU.add)
                if blk < 4:
                    TT(t, t, GB[c0 // 512][:, (c0 % 512):(c0 % 512) + 256], ALU.mult)
                    xv = Xb[blk][:, c0:c0 + 256]
                else:
                    TT(t, t, G33[l][0:NS, c0:c0 + 256], ALU.mult)
                    xv = XS[:, c0:c0 + 256]
                STT(xv, xv, ALPHA, t, ALU.mult, ALU.add)
                if i == 3:
                    final_ln(blk)

    import os
    stop = int(os.environ.get('KSTOP', '999'))

    def run_all():
        k = 0
        if stop <= k:
            return
        cond_setup()
        for l in range(DEPTH):
            load_params(l)
        for st in range(NST):
            for l in range(DEPTH):
                samp = (st == 0)
                stages = [lambda: set_layer(l)]
                if st == 0:
                    stages.append(lambda: cond_layer(l))
                stages += [lambda: build_hT(l, st, samp), lambda: lru_branch(l, st, samp),
                           lambda: rwkv_branch(l, st, samp), lambda: gmlp_branch(l, st, samp),
                           lambda: conf_branch(l, st, samp), lambda: out_stage(l, st, samp)]
                for f in stages:
                    k += 1
                    if stop <= k:
                        return
                    f()

    run_all()
    S.finish()
    return nc


_NC_CACHE = {}


def kernel(**inputs):
    inp = {k: np.ascontiguousarray(np.asarray(v, dtype=np.float32)) for k, v in inputs.items()}
    if 'nc' not in _NC_CACHE:
        _NC_CACHE['nc'] = build_nc()
    nc = _NC_CACHE['nc']
    in_maps = []
    for b in range(8):
        sl = slice(b * NS, (b + 1) * NS)
        m = {n: inp[n] for n, _ in WEIGHT_NAMES}
        m['xp'] = inp['x_prompt'][b]
        m['xs'] = np.ascontiguousarray(inp['x_sample'][sl, 0, :])
        c33 = np.zeros((NSC, D), np.float32)
        c33[0:NS] = inp['c_sample'][sl]
        c33[32] = inp['c_prompt'][b]
        m['c33'] = c33
        m['st_lc'] = np.ascontiguousarray(inp['state_lru_conv'][:, sl])
        m['st_lh'] = np.ascontiguousarray(inp['state_lru_h'][:, sl])
        m['st_sh'] = np.ascontiguousarray(inp['state_rwkv_shift'][:, sl])
        m['st_S'] = np.ascontiguousarray(inp['state_rwkv_S'][:, sl])
        m['st_cf'] = np.ascontiguousarray(inp['state_conf_conv'][:, sl])
        in_maps.append(m)
    res = run_bass_kernel_spmd(nc, in_maps, core_ids=list(range(8)))
    R = res.results
    y_p = np.stack([R[b]['o_yp'] for b in range(8)], 0)
    y_s = np.concatenate([R[b]['o_ys'] for b in range(8)], 0)[:, None, :]

    def cat_p(n):
        return np.stack([R[b][n] for b in range(8)], 1)

    def cat_s(n):
        return np.concatenate([R[b][n] for b in range(8)], 1)

    outs = (y_p, y_s, cat_p('o_lcp'), cat_s('o_lcs'), cat_p('o_lhp'), cat_s('o_lhs'),
            cat_p('o_shp'), cat_s('o_shs'), cat_p('o_Sp'), cat_s('o_Ss'),
            cat_p('o_cfp'), cat_s('o_cfs'), cat_s('o_gv')[:, :, None, :])
    return tuple(np.ascontiguousarray(o.astype(np.float32)) for o in outs)
```
